# Optimizing a Trainium2 kernel written in Bass

```python
import jax, jax.numpy as jnp
from jax import lax
import numpy as np

D_MODEL = 2048
BATCH = 2
SEQ = 8192
DEPTH = 2

HEAD_DIM = 128
POOL_WINDOWS = (2, 4, 8, 16)
POOL_WIDTH = D_MODEL // 4
POOL_GROUP = POOL_WIDTH // len(POOL_WINDOWS)
SGU_WIDTH = (D_MODEL - POOL_WIDTH) // 2
SGU_HEADS = SGU_WIDTH // HEAD_DIM
CHUNK = 128
CONV_WIDTH = D_MODEL - POOL_WIDTH - SGU_WIDTH
CONV_GROUPS = CONV_WIDTH // HEAD_DIM
CONV_KERNEL = 31
IN_WIDTH = POOL_WIDTH + 2 * SGU_WIDTH + 2 * CONV_WIDTH
D_FF = 4 * D_MODEL
DEEPNORM_ALPHA = (2 * DEPTH) ** 0.25
DEEPNORM_BETA = (8 * DEPTH) ** -0.25
LN_EPS = 1e-5

kernel_name = "hybrid_pool_sgu_conv_deepnorm"


def layer_norm(x, g, b):
    xf = x.astype(jnp.float32)
    mu = jnp.mean(xf, axis=-1, keepdims=True)
    xc = xf - mu
    var = jnp.mean(jnp.square(xc), axis=-1, keepdims=True)
    y = xc * lax.rsqrt(var + LN_EPS)
    return (y * g.astype(jnp.float32) + b.astype(jnp.float32)).astype(x.dtype)


def pool_mixer(a, w_pool, pool_scale):
    bsz, s, _ = a.shape
    cs = jnp.cumsum(a.astype(jnp.float32), axis=1)
    count = jnp.arange(1, s + 1, dtype=jnp.float32)[None, :, None]
    means = []
    for g, win in enumerate(POOL_WINDOWS):
        c = cs[..., g * POOL_GROUP:(g + 1) * POOL_GROUP]
        prev = jnp.pad(c[:, :-win], ((0, 0), (win, 0), (0, 0)))
        means.append((c - prev) / jnp.minimum(count, float(win)))
    pooled = jnp.concatenate(means, axis=-1).astype(a.dtype) - a
    pooled = pooled.reshape(bsz, s, len(POOL_WINDOWS), POOL_GROUP)
    y = jnp.einsum('bsgc,gcd->bsgd', pooled, w_pool).reshape(bsz, s, POOL_WIDTH)
    return y * pool_scale


def sgu_mixer(uv, ln_g, ln_b, w_s, b_s):
    bsz, s, _ = uv.shape
    uv = jax.nn.gelu(uv)
    u, v = jnp.split(uv, 2, axis=-1)
    v = layer_norm(v, ln_g, ln_b)
    vc = v.reshape(bsz, s // CHUNK, CHUNK, SGU_HEADS, HEAD_DIM)
    mask = jnp.tril(jnp.ones((CHUNK, CHUNK), dtype=w_s.dtype))
    mixed = jnp.einsum('hts,bnshc->bnthc', w_s * mask, vc) + b_s.T[None, None, :, :, None]
    return u * mixed.reshape(bsz, s, SGU_WIDTH)


def conv_module(ag, conv_w, conv_b, ln_g, ln_b):
    a, g = jnp.split(ag, 2, axis=-1)
    h = a * jax.nn.sigmoid(g)
    h = lax.conv_general_dilated(
        h, conv_w[:, None, :], window_strides=(1,), padding=[(CONV_KERNEL - 1, 0)],
        dimension_numbers=('NWC', 'WIO', 'NWC'), feature_group_count=CONV_WIDTH) + conv_b
    h = layer_norm(h, ln_g, ln_b)
    return jax.nn.silu(h)


def setup_inputs(seed: int = 0) -> dict:
    key = jax.random.key(seed)
    ks = jax.random.split(key, 24)

    def nrm(k, shape, scale):
        return jax.random.normal(k, shape, dtype=jnp.float32) * scale

    L = DEPTH
    return {
        "x": nrm(ks[0], (BATCH, SEQ, D_MODEL), 1.0),
        "w_in": nrm(ks[1], (L, D_MODEL, IN_WIDTH), D_MODEL ** -0.5),
        "b_in": nrm(ks[2], (L, IN_WIDTH), 0.02),
        "w_pool": nrm(ks[3], (L, len(POOL_WINDOWS), POOL_GROUP, POOL_GROUP), POOL_GROUP ** -0.5),
        "pool_scale": 1.0 + nrm(ks[4], (L, POOL_WIDTH), 0.1),
        "sgu_ln_g": 1.0 + nrm(ks[5], (L, SGU_WIDTH), 0.02),
        "sgu_ln_b": nrm(ks[6], (L, SGU_WIDTH), 0.02),
        "sgu_w": nrm(ks[7], (L, SGU_HEADS, CHUNK, CHUNK), CHUNK ** -0.5),
        "sgu_b": 1.0 + nrm(ks[8], (L, SGU_HEADS, CHUNK), 0.02),
        "conv_w": nrm(ks[9], (L, CONV_KERNEL, CONV_WIDTH), CONV_KERNEL ** -0.5),
        "conv_b": nrm(ks[10], (L, CONV_WIDTH), 0.02),
        "conv_ln_g": 1.0 + nrm(ks[11], (L, CONV_WIDTH), 0.02),
        "conv_ln_b": nrm(ks[12], (L, CONV_WIDTH), 0.02),
        "w_out": nrm(ks[13], (L, D_MODEL, D_MODEL), DEEPNORM_BETA * D_MODEL ** -0.5),
        "b_out": nrm(ks[14], (L, D_MODEL), 0.02),
        "ln1_g": 1.0 + nrm(ks[15], (L, D_MODEL), 0.02),
        "ln1_b": nrm(ks[16], (L, D_MODEL), 0.02),
        "w_ff1": nrm(ks[17], (L, D_MODEL, D_FF), D_MODEL ** -0.5),
        "b_ff1": nrm(ks[18], (L, D_FF), 0.02),
        "w_ff2": nrm(ks[19], (L, D_FF, D_MODEL), DEEPNORM_BETA * D_FF ** -0.5),
        "b_ff2": nrm(ks[20], (L, D_MODEL), 0.02),
        "ln2_g": 1.0 + nrm(ks[21], (L, D_MODEL), 0.02),
        "ln2_b": nrm(ks[22], (L, D_MODEL), 0.02),
    }


def reference(x, w_in, b_in, w_pool, pool_scale, sgu_ln_g, sgu_ln_b, sgu_w, sgu_b,
              conv_w, conv_b, conv_ln_g, conv_ln_b, w_out, b_out, ln1_g, ln1_b,
              w_ff1, b_ff1, w_ff2, b_ff2, ln2_g, ln2_b):
    for l in range(DEPTH):
        proj = jnp.einsum('bsd,de->bse', x, w_in[l]) + b_in[l]
        p_a = proj[..., :POOL_WIDTH]
        p_b = proj[..., POOL_WIDTH:POOL_WIDTH + 2 * SGU_WIDTH]
        p_c = proj[..., POOL_WIDTH + 2 * SGU_WIDTH:]
        y_a = pool_mixer(p_a, w_pool[l], pool_scale[l])
        y_b = sgu_mixer(p_b, sgu_ln_g[l], sgu_ln_b[l], sgu_w[l], sgu_b[l])
        y_c = conv_module(p_c, conv_w[l], conv_b[l], conv_ln_g[l], conv_ln_b[l])
        mixed = jnp.concatenate([y_a, y_b, y_c], axis=-1)
        mix_out = jnp.einsum('bsd,de->bse', mixed, w_out[l]) + b_out[l]
        x = layer_norm(DEEPNORM_ALPHA * x + mix_out, ln1_g[l], ln1_b[l])
        h = jnp.square(jax.nn.relu(jnp.einsum('bsd,df->bsf', x, w_ff1[l]) + b_ff1[l]))
        ff_out = jnp.einsum('bsf,fd->bsd', h, w_ff2[l]) + b_ff2[l]
        x = layer_norm(DEEPNORM_ALPHA * x + ff_out, ln2_g[l], ln2_b[l])
    return x
```

```python
import numpy as np
from contextlib import ExitStack
import concourse.bass as bass
import concourse.mybir as mybir
from concourse.bass_utils import run_bass_kernel_spmd

F32 = mybir.dt.float32
BF16 = mybir.dt.bfloat16
AF = mybir.ActivationFunctionType
ALU = mybir.AluOpType

D = 2048
DIN = 3584
DFF = 8192
NL = 2
ALPHA = float((2 * NL) ** 0.25)
EPS = 1e-5
NCORES = 8
TOK = 2048
G = 512
CX = 32
HALO = 160
SW = 256
NSLOT = 4
SPLIT = 8
WINS = (2, 4, 8, 16)

C_BIN = 0
C_PSC = 28
C_CW = 32
C_CB = 218
C_CLG = 224
C_CLB = 230
C_BOUT = 236
C_L1G = 252
C_L1B = 268
C_BF1 = 284
C_BF2 = 348
C_L2G = 364
C_L2B = 380
NPC = 396

FUSED = True


class Sched:
    def __init__(self, nc, es):
        self.nc = nc
        self.es = es
        self.E = {"pe": nc.tensor, "act": nc.scalar, "dve": nc.vector, "pool": nc.gpsimd, "sp": nc.sync}
        self.sem = {k: es.enter_context(nc.semaphore("s_" + k)) for k in self.E}
        self.cnt = {k: 0 for k in self.E}
        self.seen = {k: {} for k in self.E}
        self.lastw = {}
        self.readers = {}
        self.dsem = {}
        self.pe_open = False
        self.pe_last = None

    def _semh(self, k):
        return self.sem[k] if k in self.sem else self.dsem[k][0]

    def _wait(self, eng, ev):
        k, v = ev
        if self.seen[eng].get(k, 0) >= v:
            return
        self.seen[eng][k] = v
        self.E[eng].wait_ge(self._semh(k), v)

    def deps(self, eng, reads, writes):
        for k in reads:
            ev = self.lastw.get(k)
            if ev is not None and not (eng == "pe" and ev[0] == "pe"):
                self._wait(eng, ev)
        for k in writes:
            ev = self.lastw.get(k)
            if ev is not None and not (eng == "pe" and ev[0] == "pe"):
                self._wait(eng, ev)
            for sk, v in self.readers.get(k, {}).items():
                if eng == "pe" and sk == "pe":
                    continue
                self._wait(eng, (sk, v))

    def record(self, ev, reads, writes):
        for k in reads:
            r = self.readers.setdefault(k, {})
            if r.get(ev[0], 0) < ev[1]:
                r[ev[0]] = ev[1]
        for k in writes:
            self.lastw[k] = ev
            self.readers[k] = {}

    def op(self, eng, reads, writes, fn):
        assert not (eng == "pe")
        self.deps(eng, reads, writes)
        ins = fn(self.E[eng])
        self.cnt[eng] += 1
        ins.then_inc(self.sem[eng], 1)
        ev = (eng, self.cnt[eng])
        self.record(ev, reads, writes)
        return ev

    def mm(self, out, lhsT, rhs, reads, writes, start=True, stop=True):
        self.deps("pe", reads, writes)
        ins = self.nc.tensor.matmul(out, lhsT, rhs, start=start, stop=stop)
        ev = ("pe", self.cnt["pe"] + 1)
        self.record(ev, reads, writes)
        self.pe_last = ins
        self.pe_open = True

    def pe_end(self):
        assert self.pe_open
        self.cnt["pe"] += 1
        self.pe_last.then_inc(self.sem["pe"], 1)
        self.pe_open = False

    def new_dsem(self, name):
        self.dsem[name] = [self.es.enter_context(self.nc.semaphore("d_" + name)), 0]

    def dma(self, eng, semname, out, in_, reads, writes):
        assert not self.pe_open or eng != "pe"
        self.deps(eng, reads, writes)
        ins = self.E[eng].dma_start(out=out, in_=in_)
        s = self.dsem[semname]
        s[1] += 16
        ins.then_inc(s[0], 16)
        ev = (semname, s[1])
        self.record(ev, reads, writes)
        return ev


def build_program(layers, cxh):
    fused = len(layers) == 2
    nc = bass.Bass("TRN2", target_bir_lowering=False)
    es = ExitStack()
    S = Sched(nc, es)

    def dram(name, shape, kind="ExternalInput"):
        return nc.dram_tensor(name, list(shape), F32, kind=kind).ap()

    x_in = dram("x_core", [cxh + TOK, D])
    out_d = dram("out_core", [TOK, D], kind="ExternalOutput")
    W = {}
    for l in layers:
        W[l] = dict(
            w_in=dram(f"w_in{l}", [D, DIN]), w_out=dram(f"w_out{l}", [D, D]),
            w_ff1=dram(f"w_ff1{l}", [D, DFF]), w_ff2=dram(f"w_ff2{l}", [DFF, D]),
            pcols=dram(f"pcols{l}", [128, NPC]), rows=dram(f"rows{l}", [128, 3, 768]),
            bsb=dram(f"bsb{l}", [128, 6, 128]), wT=dram(f"wT{l}", [128, 6, 128]),
            wpool=dram(f"wpool{l}", [128, 4, 128]),
        )
    ident_d = dram("ident", [128, 128])
    maskT_d = dram("maskT", [128, 128])
    rc16_d = dram("rc16", [128, 4, 16])
    cmask_d = dram("cmask", [128, 1])

    def sb(name, shape, dt=F32):
        return es.enter_context(nc.sbuf_tensor("sb_" + name, list(shape), dt))

    xres = sb("xres", [128, 16, G])
    xT = sb("xT", [128, 16, G], BF16)
    mT = sb("mT", [128, 16, G], BF16)
    slabs = [sb(f"slab{i}", [128, 16, SW], BF16) for i in range(NSLOT)]
    stage = [sb(f"stage{i}", [128, 1024]) for i in range(2)]
    pa = sb("pa", [128, 4, CX + G])
    pooled = [sb(f"pooled{i}", [128, G], BF16) for i in range(2)]
    sA = sb("sA", [128, CX + G])
    sB = sb("sB", [128, CX + G])
    ub = sb("ub", [128, 6, G])
    vraw = sb("vraw", [128, 4, 768])
    vnb = [sb(f"vnb{i}", [128, 768], BF16) for i in range(2)]
    hglu = sb("hglu", [128, 6, CX + G])
    sig = [sb(f"sig{i}", [128, G]) for i in range(2)]
    acc = sb("acc", [128, 6, G])
    zb = sb("zb", [128, G], BF16)
    zsq = sb("zsq", [128, G], BF16)
    stA = sb("stA", [128, G])
    stB = sb("stB", [128, G])
    stC = sb("stC", [128, G])
    SD = nc.vector.BN_STATS_DIM
    bnst = sb("bnst", [128, 2, SD])
    bnag = sb("bnag", [128, 4])
    pactx = {l: sb(f"pactx{l}", [128, 4, CX]) for l in layers}
    hctx = {l: sb(f"hctx{l}", [128, 6, CX]) for l in layers}
    pcols = {l: sb(f"pcols{l}", [128, NPC]) for l in layers}
    pca = {l: sb(f"pca{l}", [128, 64]) for l in layers}
    rows = sb("rows", [128, 3, 768])
    bsb = sb("bsb", [128, 6, 128])
    wTraw = sb("wTraw", [128, 6, 128], BF16)
    wTm = sb("wTm", [128, 6, 128], BF16)
    wpool = sb("wpool", [128, 4, 128], BF16)
    ident = sb("ident", [128, 128])
    maskT = sb("maskT", [128, 128], BF16)
    ones = sb("ones", [128, 128], BF16)
    rc16 = sb("rc16", [128, 4, 16])
    cmask = sb("cmask", [128, 1])
    epsc = sb("epsc", [128, 1])
    zc = sb("zc", [128, 1])
    ps = [es.enter_context(nc.psum_tensor(f"ps{i}", [128, 512], F32)) for i in range(8)]
    print("sbuf bytes remaining/partition:", nc.sbuf_bytes_remaining)

    for i in range(NSLOT):
        S.new_dsem(f"slab{i}")
    for n in ["setup", "setup_p", "mixc", "mixc_p", "st0", "st1", "out0", "out1"]:
        S.new_dsem(n)

    bank_ctr = [0]

    def bank():
        b = bank_ctr[0] % 8
        bank_ctr[0] += 1
        return b

    units = []
    if fused:
        units.append(dict(kind="halo", row0=0, N=128, cx=CX))
    else:
        units.append(dict(kind="ctx", row0=0, N=CX, cx=0))
    for g in range(TOK // G):
        units.append(dict(kind="main", row0=cxh + g * G, N=G, cx=0, g=g))

    INP_ORDER = [0, 2, 16, 18, 20, 22, 24, 26, 4, 6, 8, 10, 12, 14]
    CTX_ORDER = [0, 2, 16, 18, 20, 22, 24, 26]

    def layer_slabs(l, ctx_only):
        lst = []
        wv = W[l]["w_in"].rearrange("(kc p) c -> p kc c", p=128)
        for c0 in (CTX_ORDER if ctx_only else INP_ORDER):
            lst.append(("in", l, c0, wv[:, :, c0 * 128:c0 * 128 + SW]))
        if ctx_only:
            return lst
        wv = W[l]["w_out"].rearrange("(kc p) c -> p kc c", p=128)
        for c0 in range(0, 16, 2):
            lst.append(("out", l, c0, wv[:, :, c0 * 128:c0 * 128 + SW]))
        w1 = W[l]["w_ff1"].rearrange("(kc p) c -> p kc c", p=128)
        w2 = W[l]["w_ff2"].rearrange("(kc p) c -> p kc c", p=128)
        for q in range(4):
            for c0 in range(0, 16, 2):
                f0 = q * 16 + c0
                lst.append(("ff1", l, f0, w1[:, :, f0 * 128:f0 * 128 + SW]))
            for c0 in range(0, 16, 2):
                lst.append(("ff2", l, (q, c0), w2[:, q * 16:(q + 1) * 16, c0 * 128:c0 * 128 + SW]))
        return lst

    slab_list = []
    for u in units:
        if u["kind"] == "halo":
            slab_list += layer_slabs(layers[0], False) + layer_slabs(layers[1], True)
        elif u["kind"] == "ctx":
            slab_list += layer_slabs(layers[0], True)
        else:
            for l in layers:
                slab_list += layer_slabs(l, False)
    slab_state = dict(next=0, issued=0)

    def issue_slabs(upto):
        while slab_state["issued"] <= upto and slab_state["issued"] < len(slab_list):
            j = slab_state["issued"]
            slot = j % NSLOT
            key = ("slab", slot)
            S.deps("pool", [], [key])
            sem = S.dsem[f"slab{slot}"]
            for k2 in range(0, 16, SPLIT):
                ins = nc.gpsimd.dma_start(out=slabs[slot][:, k2:k2 + SPLIT, :], in_=slab_list[j][3][:, k2:k2 + SPLIT, :])
                sem[1] += 16
                ins.then_inc(sem[0], 16)
            S.record((f"slab{slot}", sem[1]), [], [key])
            slab_state["issued"] += 1

    def get_slab(kind, l, tag):
        i = slab_state["next"]
        d = slab_list[i]
        assert d[0] == kind and d[1] == l and d[2] == tag, (d[:3], kind, l, tag)
        slab_state["next"] += 1
        issue_slabs(i + NSLOT - 1)
        return i % NSLOT

    setup_keys = []

    def setup_dma(eng, out, in_, key):
        sn = "setup" if eng == "sp" else "setup_p"
        S.dma(eng, sn, out, in_, [], [key])
        setup_keys.append((key, sn))

    setup_dma("sp", ident[:], ident_d, "ident")
    setup_dma("sp", rc16[:], rc16_d, "rc16")
    setup_dma("sp", cmask[:], cmask_d, "cmask")
    for l in layers:
        setup_dma("sp", pcols[l][:], W[l]["pcols"], ("pcols", l))
    setup_dma("pool", maskT[:], maskT_d, "maskT")
    for k, sn in setup_keys:
        S.lastw[k] = (sn, S.dsem[sn][1])
    S.op("dve", [], ["ones"], lambda e: e.memset(ones[:], 1.0))
    S.op("dve", [], ["epsc"], lambda e: e.memset(epsc[:], EPS))
    S.op("dve", [], ["zc"], lambda e: e.memset(zc[:], 0.0))
    for l in layers:
        for (src, dst) in ((C_L1G, 0), (C_L1B, 16), (C_L2G, 32), (C_L2B, 48)):
            S.op("dve", [("pcols", l)], [("pca", l)],
                 lambda e, l=l, src=src, dst=dst: e.tensor_scalar(
                     out=pca[l][:, dst:dst + 16], in0=pcols[l][:, src:src + 16],
                     scalar1=ALPHA, scalar2=None, op0=ALU.mult))
    issue_slabs(NSLOT - 1)

    mixc_state = dict(layer=None)

    def load_mix_consts(l):
        if mixc_state["layer"] == l:
            return
        mixc_state["layer"] = l
        S.dma("sp", "mixc", rows[:], W[l]["rows"], [], ["rows"])
        S.dma("sp", "mixc", bsb[:], W[l]["bsb"], [], ["bsb"])
        S.dma("pool", "mixc_p", wTraw[:], W[l]["wT"], [], ["wTraw"])
        S.dma("pool", "mixc_p", wpool[:], W[l]["wpool"], [], ["wpool"])
        for k in ("rows", "bsb"):
            S.lastw[k] = ("mixc", S.dsem["mixc"][1])
        for k in ("wTraw", "wpool"):
            S.lastw[k] = ("mixc_p", S.dsem["mixc_p"][1])
        S.op("dve", ["wTraw", "maskT"], ["wTm"],
             lambda e: e.tensor_tensor(out=wTm[:], in0=wTraw[:],
                                       in1=maskT[:].unsqueeze(1).broadcast_to([128, 6, 128]), op=ALU.mult))

    def pc(l, col):
        return pcols[l][:, col:col + 1]

    def ln_fm(srcs, Dn, N, out_fn, ring=False):
        bs_, bq_ = bank(), bank()
        n = len(srcs)
        if ring:
            zbr = [(("xT", j), xT[:, j, 0:N]) for j in range(8)]
            zqr = [(("xT", 8 + j), xT[:, 8 + j, 0:N]) for j in range(8)]
        else:
            zbr = [("zb", zb[:, 0:N])]
            zqr = [("zsq", zsq[:, 0:N])]
        depth = len(zbr)

        def conv_ops(i):
            k, ap = srcs[i]
            zk, zap = zbr[i % depth]
            qk, qap = zqr[i % depth]
            S.op("act", [k], [zk], lambda e: e.activation(out=zap, in_=ap, func=AF.Copy, bias=0.0, scale=1.0))
            S.op("act", [k], [qk], lambda e: e.activation(out=qap, in_=ap, func=AF.Square, bias=0.0, scale=1.0))

        def mm_ops(i):
            zk, zap = zbr[i % depth]
            qk, qap = zqr[i % depth]
            S.mm(ps[bs_][:, 0:N], ones[:], zap, [zk, "ones"], [("ps", bs_)], start=(i == 0), stop=(i == n - 1))
            S.pe_end()
            S.mm(ps[bq_][:, 0:N], ones[:], qap, [qk, "ones"], [("ps", bq_)], start=(i == 0), stop=(i == n - 1))
            S.pe_end()

        for i in range(min(depth, n)):
            conv_ops(i)
        for i in range(n):
            mm_ops(i)
            if i + depth < n:
                conv_ops(i + depth)
        inv = 1.0 / Dn
        S.op("dve", [("ps", bs_)], ["stA"], lambda e: e.tensor_scalar(out=stA[:, 0:N], in0=ps[bs_][:, 0:N], scalar1=inv, scalar2=None, op0=ALU.mult))
        S.op("dve", [("ps", bq_)], ["stB"], lambda e: e.tensor_scalar(out=stB[:, 0:N], in0=ps[bq_][:, 0:N], scalar1=inv, scalar2=None, op0=ALU.mult))
        S.op("dve", ["stA"], ["stC"], lambda e: e.tensor_tensor(out=stC[:, 0:N], in0=stA[:, 0:N], in1=stA[:, 0:N], op=ALU.mult))
        S.op("dve", ["stB", "stC"], ["stB"], lambda e: e.tensor_tensor(out=stB[:, 0:N], in0=stB[:, 0:N], in1=stC[:, 0:N], op=ALU.subtract))
        S.op("act", ["stB", "epsc"], ["stB"], lambda e: e.activation(out=stB[:, 0:N], in_=stB[:, 0:N], func=AF.Sqrt, bias=epsc[:, 0:1], scale=1.0))
        S.op("dve", ["stB"], ["stB"], lambda e: e.reciprocal(out=stB[:, 0:N], in_=stB[:, 0:N]))
        S.op("dve", ["stA", "stB"], ["stA"], lambda e: e.scalar_tensor_tensor(out=stA[:, 0:N], in0=stA[:, 0:N], scalar=-1.0, in1=stB[:, 0:N], op0=ALU.mult, op1=ALU.mult))
        for i, (k, ap) in enumerate(srcs):
            S.op("dve", [k, "stB"], [k], lambda e, ap=ap: e.tensor_tensor(out=ap, in0=ap, in1=stB[:, 0:N], op=ALU.mult))
        for i, (k, ap) in enumerate(srcs):
            S.op("dve", [k, "stA"], [k], lambda e, ap=ap: e.tensor_tensor(out=ap, in0=ap, in1=stA[:, 0:N], op=ALU.add))
        for i in range(n):
            out_fn(i)

    def load_x(u):
        ntok = u["cx"] + u["N"]
        t0 = 0
        while t0 < ntok:
            tn = min(128, ntok - t0)
            for h in range(2):
                S.dma("sp", f"st{h}", stage[h][0:tn, :], x_in[u["row0"] + t0:u["row0"] + t0 + tn, h * 1024:(h + 1) * 1024],
                      [], [("stage", h)])
                for b4 in range(2):
                    b = bank()
                    c0 = h * 8 + b4 * 4
                    for j in range(4):
                        S.mm(ps[b][:, j * 128:j * 128 + tn], stage[h][0:tn, (b4 * 4 + j) * 128:(b4 * 4 + j + 1) * 128],
                             ident[0:tn, 0:tn], [("stage", h), "ident"], [("ps", b)])
                    S.pe_end()
                    pv = ps[b][:].rearrange("p (j t) -> p j t", j=4)[:, :, 0:tn]
                    ck = [("xres", c0 + j) for j in range(4)]
                    tk = [("xT", c0 + j) for j in range(4)]
                    S.op("act", [("ps", b)], tk, lambda e, pv=pv, c0=c0, t0=t0, tn=tn: e.activation(
                        out=xT[:, c0:c0 + 4, t0:t0 + tn], in_=pv, func=AF.Copy, bias=0.0, scale=1.0))
                    lo = max(t0, u["cx"])
                    if lo < t0 + tn:
                        pvr = ps[b][:].rearrange("p (j t) -> p j t", j=4)[:, :, lo - t0:tn]
                        S.op("dve", [("ps", b)] + tk, ck, lambda e, pvr=pvr, c0=c0, lo=lo, t0=t0, tn=tn: e.tensor_scalar(
                            out=xres[:, c0:c0 + 4, lo - u["cx"]:t0 + tn - u["cx"]], in0=pvr, scalar1=ALPHA, scalar2=None, op0=ALU.mult))
            t0 += tn

    def store_out(u):
        N = u["N"]
        orow = u["row0"] - cxh
        for tt in range(N // 128):
            for h in range(2):
                for b4 in range(2):
                    b = bank()
                    c0 = h * 8 + b4 * 4
                    for j in range(4):
                        S.mm(ps[b][:, j * 128:(j + 1) * 128], xres[:, c0 + j, tt * 128:(tt + 1) * 128], ident[:],
                             [("xres", c0 + j), "ident"], [("ps", b)])
                    S.pe_end()
                    S.op("act", [("ps", b)], [("stage", h)], lambda e, b=b, h=h, b4=b4: e.activation(
                        out=stage[h][:, b4 * 512:(b4 + 1) * 512], in_=ps[b][:], func=AF.Copy, bias=0.0, scale=1.0))
                S.dma("sp", f"out{h}", out_d[orow + tt * 128:orow + (tt + 1) * 128, h * 1024:(h + 1) * 1024], stage[h][:],
                      [("stage", h)], [("outd", tt, h, u["g"])])

    def proj_batch(slot, j, src, N, col0, nk=16):
        b = bank()
        for kc in range(nk):
            S.mm(ps[b][:, 0:N], slabs[slot][:, kc, j * 128:(j + 1) * 128], src[:, kc, col0:col0 + N],
                 [("slab", slot), (src_key[id(src)], kc)], [("ps", b)], start=(kc == 0), stop=(kc == nk - 1))
        S.pe_end()
        return b

    src_key = {id(xT): "xT", id(mT): "mT"}

    def mix_inproj(l, u, ctx_only):
        N, cx = u["N"], u["cx"]
        if ctx_only:
            ncol, xcol0, dcol0 = CX, u["xcol0"], CX
        else:
            ncol, xcol0, dcol0 = cx + N, 0, CX - cx
        if not ctx_only:
            load_mix_consts(l)
            if cx == 0:
                S.op("dve", [("pactx", l)], [("pa", i) for i in range(4)],
                     lambda e: e.tensor_copy(out=pa[:, :, 0:CX], in_=pactx[l][:]))
                S.op("dve", [("hctx", l)], [("hglu", i) for i in range(6)],
                     lambda e: e.tensor_copy(out=hglu[:, :, 0:CX], in_=hctx[l][:]))
        order = CTX_ORDER if ctx_only else INP_ORDER
        for c0 in order:
            slot = get_slab("in", l, c0)
            if 10 <= c0 < 16:
                vc0 = (c0 - 10) * 128
                for tt in range(N // 128):
                    b = bank()
                    for kc in range(16):
                        S.mm(ps[b][:, 0:SW], xT[:, kc, cx + tt * 128:cx + (tt + 1) * 128], slabs[slot][:, kc, :],
                             [("slab", slot), ("xT", kc)], [("ps", b)], start=(kc == 0), stop=(kc == 15))
                    S.pe_end()
                    S.op("dve", [("ps", b), "rows"], [("vraw", tt)], lambda e, b=b, tt=tt, vc0=vc0: e.tensor_tensor(
                        out=vraw[:, tt, vc0:vc0 + SW], in0=ps[b][:, 0:SW], in1=rows[:, 0, vc0:vc0 + SW], op=ALU.add))
                if c0 == 14:
                    sgu_path(l, u)
                continue
            for j in range(2):
                c = c0 + j
                if c < 4:
                    b = proj_batch(slot, j, xT, ncol, xcol0)
                    S.op("act", [("ps", b), ("pcols", l)], [("pa", c)], lambda e, b=b, c=c: e.activation(
                        out=pa[:, c, dcol0:dcol0 + ncol], in_=ps[b][:, 0:ncol], func=AF.Identity, bias=pc(l, C_BIN + c), scale=1.0))
                elif c < 10:
                    i = c - 4
                    b = proj_batch(slot, j, xT, N, cx)
                    S.op("act", [("ps", b), ("pcols", l)], [("ub", i)], lambda e, b=b, c=c, i=i: e.activation(
                        out=ub[:, i, 0:N], in_=ps[b][:, 0:N], func=AF.Gelu_apprx_tanh, bias=pc(l, C_BIN + c), scale=1.0))
                elif c < 22:
                    i = c - 16
                    b = proj_batch(slot, j, xT, ncol, xcol0)
                    S.op("act", [("ps", b), ("pcols", l)], [("hglu", i)], lambda e, b=b, c=c, i=i: e.activation(
                        out=hglu[:, i, dcol0:dcol0 + ncol], in_=ps[b][:, 0:ncol], func=AF.Identity, bias=pc(l, C_BIN + c), scale=1.0))
                else:
                    i = c - 22
                    b = proj_batch(slot, j, xT, ncol, xcol0)
                    sg = sig[i % 2]
                    sk = ("sig", i % 2)
                    S.op("act", [("ps", b), ("pcols", l)], [sk], lambda e, b=b, c=c, sg=sg: e.activation(
                        out=sg[:, 0:ncol], in_=ps[b][:, 0:ncol], func=AF.Sigmoid, bias=pc(l, C_BIN + c), scale=1.0))
                    S.op("dve", [sk, ("hglu", i)], [("hglu", i)], lambda e, i=i, sg=sg: e.tensor_tensor(
                        out=hglu[:, i, dcol0:dcol0 + ncol], in0=hglu[:, i, dcol0:dcol0 + ncol], in1=sg[:, 0:ncol], op=ALU.mult))
            if c0 == 2:
                if ctx_only:
                    save_ctx_pa(l, CX, True)
                else:
                    pool_path(l, u)
            if c0 == 20 and not ctx_only:
                pool_pe(l, u)
            if c0 == 26:
                if ctx_only:
                    save_ctx_h(l, CX, True)
                else:
                    conv_path(l, u)

    def save_ctx_pa(l, N, masked):
        rk = [("pa", i) for i in range(4)]
        if masked:
            S.op("dve", rk + ["cmask"], [("pactx", l)], lambda e: e.tensor_scalar(
                out=pactx[l][:], in0=pa[:, :, N:N + CX], scalar1=cmask[:, 0:1], scalar2=None, op0=ALU.mult))
        else:
            S.op("dve", rk, [("pactx", l)], lambda e: e.tensor_copy(out=pactx[l][:], in_=pa[:, :, N:N + CX]))

    def save_ctx_h(l, N, masked):
        rk = [("hglu", i) for i in range(6)]
        if masked:
            S.op("dve", rk + ["cmask"], [("hctx", l)], lambda e: e.tensor_scalar(
                out=hctx[l][:], in0=hglu[:, :, N:N + CX], scalar1=cmask[:, 0:1], scalar2=None, op0=ALU.mult))
        else:
            S.op("dve", rk, [("hctx", l)], lambda e: e.tensor_copy(out=hctx[l][:], in_=hglu[:, :, N:N + CX]))

    pbufs = [(pooled[0], ("pooled", 0)), (pooled[1], ("pooled", 1)), (zb, "zb"), (zsq, "zsq")]

    def pool_path(l, u):
        N = u["N"]
        first_main = (u["kind"] == "main" and u["g"] == 0)
        E = CX + N
        for gi, win in enumerate(WINS):
            a = pa[:, gi, :]
            ak = ("pa", gi)
            bufs = [(sA, "sA"), (sB, "sB")]
            src, srck = a, ak
            lo = CX - (win - 1)
            step = 1
            bi = 0
            while step < win:
                dst, dstk = bufs[bi]
                lo2 = lo + step
                S.op("dve", [srck], [dstk], lambda e, dst=dst, src=src, lo2=lo2, step=step: e.tensor_tensor(
                    out=dst[:, lo2:E], in0=src[:, lo2:E], in1=src[:, lo2 - step:E - step], op=ALU.add))
                src, srck = dst, dstk
                lo = lo2
                step *= 2
                bi ^= 1
            pb, pk = pbufs[gi]
            S.op("dve", [srck, ak], [pk], lambda e, pb=pb, src=src, a=a, win=win: e.scalar_tensor_tensor(
                out=pb[:, 0:N], in0=src[:, CX:E], scalar=1.0 / win, in1=a[:, CX:E], op0=ALU.mult, op1=ALU.subtract))
            if first_main:
                S.op("dve", [srck, "rc16"], ["stC"], lambda e, src=src, gi=gi: e.tensor_tensor(
                    out=stC[:, 0:16], in0=src[:, CX:CX + 16], in1=rc16[:, gi, :], op=ALU.mult))
                S.op("dve", ["stC", ak], [pk], lambda e, pb=pb, a=a: e.tensor_tensor(
                    out=pb[:, 0:16], in0=stC[:, 0:16], in1=a[:, CX:CX + 16], op=ALU.subtract))
        save_ctx_pa(l, N, u["kind"] == "halo")

    def pool_pe(l, u):
        N = u["N"]
        for gi in range(4):
            pb, pk = pbufs[gi]
            b = bank()
            S.mm(ps[b][:, 0:N], wpool[:, gi, :], pb[:, 0:N], ["wpool", pk], [("ps", b)])
            S.pe_end()
            S.op("act", [("ps", b), ("pcols", l), "zc"], [("mT", gi)], lambda e, b=b, gi=gi: e.activation(
                out=mT[:, gi, 0:N], in_=ps[b][:, 0:N], func=AF.Identity, bias=zc[:, 0:1], scale=pc(l, C_PSC + gi)))

    def conv_path(l, u):
        N = u["N"]
        for k in range(31):
            for i in range(6):
                wcol = pcols[l][:, C_CW + i * 31 + k:C_CW + i * 31 + k + 1]
                if k == 0:
                    S.op("dve", [("hglu", i), ("pcols", l)], [("acc", i)], lambda e, i=i, wcol=wcol: e.tensor_scalar(
                        out=acc[:, i, 0:N], in0=hglu[:, i, 2:2 + N], scalar1=wcol, scalar2=pc(l, C_CB + i),
                        op0=ALU.mult, op1=ALU.add))
                else:
                    S.op("dve", [("hglu", i), ("pcols", l), ("acc", i)], [("acc", i)], lambda e, i=i, wcol=wcol, k=k: e.scalar_tensor_tensor(
                        out=acc[:, i, 0:N], in0=hglu[:, i, 2 + k:2 + k + N], scalar=wcol, in1=acc[:, i, 0:N],
                        op0=ALU.mult, op1=ALU.add))
        save_ctx_h(l, N, u["kind"] == "halo")

        def outf(i):
            S.op("act", [("acc", i), ("pcols", l)], [("mT", 10 + i)], lambda e, i=i: e.activation(
                out=mT[:, 10 + i, 0:N], in_=acc[:, i, 0:N], func=AF.Silu, bias=pc(l, C_CLB + i), scale=pc(l, C_CLG + i)))
        ln_fm([(("acc", i), acc[:, i, 0:N]) for i in range(6)], 768.0, N, outf)

    def sgu_path(l, u):
        N = u["N"]
        ntt = N // 128
        for tt in range(ntt):
            vk = ("vraw", tt)
            v = vraw[:, tt, :]
            S.op("act", [vk], [vk], lambda e, v=v: e.activation(out=v, in_=v, func=AF.Gelu_apprx_tanh, bias=0.0, scale=1.0))
            S.op("dve", [vk], ["bnst"], lambda e, v=v: e.bn_stats(out=bnst[:, 0, :], in_=v[:, 0:384]))
            S.op("dve", [vk], ["bnst"], lambda e, v=v: e.bn_stats(out=bnst[:, 1, :], in_=v[:, 384:768]))
            S.op("dve", ["bnst"], ["bnag"], lambda e: e.bn_aggr(out=bnag[:, 0:2], in_=bnst[:]))
            S.op("act", ["bnag", "epsc"], ["bnag2"], lambda e: e.activation(out=bnag[:, 2:3], in_=bnag[:, 1:2], func=AF.Sqrt, bias=epsc[:, 0:1], scale=1.0))
            S.op("dve", ["bnag2"], ["bnag3"], lambda e: e.reciprocal(out=bnag[:, 3:4], in_=bnag[:, 2:3]))
            S.op("dve", [vk, "bnag", "bnag3"], [vk], lambda e, v=v: e.tensor_scalar(
                out=v, in0=v, scalar1=bnag[:, 0:1], scalar2=bnag[:, 3:4], op0=ALU.subtract, op1=ALU.mult))
            S.op("dve", [vk, "rows"], [vk], lambda e, v=v: e.tensor_tensor(out=v, in0=v, in1=rows[:, 1, :], op=ALU.mult))
            vb = vnb[tt % 2]
            vbk = ("vnb", tt % 2)
            S.op("dve", [vk, "rows"], [vbk], lambda e, v=v, vb=vb: e.tensor_tensor(out=vb[:], in0=v, in1=rows[:, 2, :], op=ALU.add))
            for h in range(6):
                b = bank()
                S.mm(ps[b][:, 0:128], vb[:, h * 128:(h + 1) * 128], wTm[:, h, :], [vbk, "wTm"], [("ps", b)])
                S.pe_end()
                S.op("dve", [("ps", b), "bsb"], ["stC"], lambda e, b=b, h=h: e.tensor_tensor(
                    out=stC[:, 0:128], in0=ps[b][:, 0:128], in1=bsb[:, h, :], op=ALU.add))
                S.op("dve", ["stC", ("ub", h)], [("mT", 4 + h)], lambda e, h=h, tt=tt: e.tensor_tensor(
                    out=mT[:, 4 + h, tt * 128:(tt + 1) * 128], in0=stC[:, 0:128], in1=ub[:, h, tt * 128:(tt + 1) * 128], op=ALU.mult))

    def out_proj_ln1(l, u):
        N = u["N"]
        for c0 in range(0, 16, 2):
            slot = get_slab("out", l, c0)
            for j in range(2):
                c = c0 + j
                b = proj_batch(slot, j, mT, N, 0)
                S.op("dve", [("ps", b), ("pcols", l), ("xres", c)], [("xres", c)], lambda e, b=b, c=c: e.scalar_tensor_tensor(
                    out=xres[:, c, 0:N], in0=ps[b][:, 0:N], scalar=pc(l, C_BOUT + c), in1=xres[:, c, 0:N], op0=ALU.add, op1=ALU.add))

        def outf(i):
            S.op("act", [("xres", i), ("pcols", l)], [("xT", i)], lambda e, i=i: e.activation(
                out=xT[:, i, 0:N], in_=xres[:, i, 0:N], func=AF.Identity, bias=pc(l, C_L1B + i), scale=pc(l, C_L1G + i)))
            S.op("dve", [("xres", i), ("pca", l)], [("xres", i)], lambda e, i=i: e.tensor_scalar(
                out=xres[:, i, 0:N], in0=xres[:, i, 0:N], scalar1=pca[l][:, i:i + 1], scalar2=pca[l][:, 16 + i:17 + i],
                op0=ALU.mult, op1=ALU.add))
        ln_fm([(("xres", i), xres[:, i, 0:N]) for i in range(16)], float(D), N, outf, ring=True)

    def ffn_ln2(l, u, final, need_res):
        N = u["N"]
        for q in range(4):
            for c0 in range(0, 16, 2):
                f0 = q * 16 + c0
                slot = get_slab("ff1", l, f0)
                for j in range(2):
                    f = f0 + j
                    b = proj_batch(slot, j, xT, N, 0)
                    sg = sig[f % 2]
                    sk = ("sig", f % 2)
                    S.op("act", [("ps", b), ("pcols", l)], [sk], lambda e, b=b, f=f, sg=sg: e.activation(
                        out=sg[:, 0:N], in_=ps[b][:, 0:N], func=AF.Relu, bias=pc(l, C_BF1 + f), scale=1.0))
                    S.op("dve", [sk], [("mT", c0 + j)], lambda e, sg=sg, c0=c0, j=j: e.tensor_tensor(
                        out=mT[:, c0 + j, 0:N], in0=sg[:, 0:N], in1=sg[:, 0:N], op=ALU.mult))
            for c0 in range(0, 16, 2):
                slot = get_slab("ff2", l, (q, c0))
                for j in range(2):
                    c = c0 + j
                    b = proj_batch(slot, j, mT, N, 0)
                    if q == 0:
                        S.op("dve", [("ps", b), ("pcols", l), ("xres", c)], [("xres", c)], lambda e, b=b, c=c: e.scalar_tensor_tensor(
                            out=xres[:, c, 0:N], in0=ps[b][:, 0:N], scalar=pc(l, C_BF2 + c), in1=xres[:, c, 0:N], op0=ALU.add, op1=ALU.add))
                    else:
                        S.op("dve", [("ps", b), ("xres", c)], [("xres", c)], lambda e, b=b, c=c: e.tensor_tensor(
                            out=xres[:, c, 0:N], in0=ps[b][:, 0:N], in1=xres[:, c, 0:N], op=ALU.add))

        def outf(i):
            if final:
                S.op("dve", [("xres", i), ("pcols", l)], [("xres", i)], lambda e, i=i: e.tensor_scalar(
                    out=xres[:, i, 0:N], in0=xres[:, i, 0:N], scalar1=pc(l, C_L2G + i), scalar2=pc(l, C_L2B + i),
                    op0=ALU.mult, op1=ALU.add))
            else:
                S.op("act", [("xres", i), ("pcols", l)], [("xT", i)], lambda e, i=i: e.activation(
                    out=xT[:, i, 0:N], in_=xres[:, i, 0:N], func=AF.Identity, bias=pc(l, C_L2B + i), scale=pc(l, C_L2G + i)))
                if need_res:
                    S.op("dve", [("xres", i), ("pca", l)], [("xres", i)], lambda e, i=i: e.tensor_scalar(
                        out=xres[:, i, 0:N], in0=xres[:, i, 0:N], scalar1=pca[l][:, 32 + i:33 + i], scalar2=pca[l][:, 48 + i:49 + i],
                        op0=ALU.mult, op1=ALU.add))
        ln_fm([(("xres", i), xres[:, i, 0:N]) for i in range(16)], float(D), N, outf, ring=True)

    import os
    KSTOP = int(os.environ.get("KSTOP", "0"))
    class StopEmit(Exception):
        pass
    stage_ctr = [0]
    def ckpt():
        stage_ctr[0] += 1
        if KSTOP and stage_ctr[0] >= KSTOP:
            raise StopEmit()
    try:
      for u in units:
        load_x(u)
        ckpt()
        if u["kind"] == "ctx":
            u2 = dict(u)
            u2["xcol0"] = 0
            mix_inproj(layers[0], u2, True)
            ckpt()
            continue
        if u["kind"] == "halo":
            l = layers[0]
            mix_inproj(l, u, False)
            out_proj_ln1(l, u)
            ffn_ln2(l, u, False, False)
            u2 = dict(u)
            u2["xcol0"] = u["N"] - CX
            mix_inproj(layers[1], u2, True)
            continue
        for li, l in enumerate(layers):
            last = (li == len(layers) - 1)
            mix_inproj(l, u, False)
            ckpt()
            out_proj_ln1(l, u)
            ckpt()
            ffn_ln2(l, u, last, True)
            ckpt()
        store_out(u)
        ckpt()
      assert slab_state["next"] == len(slab_list), (slab_state, len(slab_list))
    except StopEmit:
        if S.pe_open:
            S.pe_end()
        for k in ("pe", "act", "dve", "pool"):
            if S.cnt[k]:
                nc.sync.wait_ge(S.sem[k], S.cnt[k])
        for k, v in S.dsem.items():
            if v[1]:
                nc.sync.wait_ge(v[0], v[1])
    for h in range(2):
        nc.sync.wait_ge(S.dsem[f"out{h}"][0], S.dsem[f"out{h}"][1])
    es.close()
    return nc


def _host_layer_inputs(inp, l):
    f = np.float32
    cols = []
    cols.append(inp["b_in"][l].reshape(28, 128).T)
    cols.append(inp["pool_scale"][l].reshape(4, 128).T)
    cols.append(inp["conv_w"][l].T.reshape(6, 128, 31).transpose(1, 0, 2).reshape(128, 186))
    for n, k in (("conv_b", 6), ("conv_ln_g", 6), ("conv_ln_b", 6), ("b_out", 16), ("ln1_g", 16), ("ln1_b", 16),
                 ("b_ff1", 64), ("b_ff2", 16), ("ln2_g", 16), ("ln2_b", 16)):
        cols.append(inp[n][l].reshape(k, 128).T)
    pcols = np.ascontiguousarray(np.concatenate(cols, axis=1), dtype=f)
    assert pcols.shape == (128, NPC)
    rows = np.stack([inp["b_in"][l][1280:2048], inp["sgu_ln_g"][l], inp["sgu_ln_b"][l]], axis=0)
    rows = np.ascontiguousarray(np.broadcast_to(rows[None], (128, 3, 768)), dtype=f)
    bsb = np.ascontiguousarray(np.broadcast_to(inp["sgu_b"][l][None], (128, 6, 128)), dtype=f)
    wT = np.ascontiguousarray(inp["sgu_w"][l].transpose(2, 0, 1), dtype=f)
    wpool = np.ascontiguousarray(inp["w_pool"][l].transpose(1, 0, 2), dtype=f)
    return {
        f"w_in{l}": np.ascontiguousarray(inp["w_in"][l]), f"w_out{l}": np.ascontiguousarray(inp["w_out"][l]),
        f"w_ff1{l}": np.ascontiguousarray(inp["w_ff1"][l]), f"w_ff2{l}": np.ascontiguousarray(inp["w_ff2"][l]),
        f"pcols{l}": pcols, f"rows{l}": rows, f"bsb{l}": bsb, f"wT{l}": wT, f"wpool{l}": wpool,
    }


def _core_consts(seg):
    ident = np.eye(128, dtype=np.float32)
    maskT = np.triu(np.ones((128, 128), dtype=np.float32))
    rc = np.zeros((128, 4, 16), dtype=np.float32)
    for gi, w in enumerate(WINS):
        for j in range(16):
            rc[:, gi, j] = 1.0 / (min(j + 1, w) if seg == 0 else w)
    cm = np.full((128, 1), 0.0 if seg == 0 else 1.0, dtype=np.float32)
    return {"ident": ident, "maskT": maskT, "rc16": rc, "cmask": cm}


def _run(layers, cxh, xfull, inp):
    nc = build_program(layers, cxh)
    shared = {}
    for l in layers:
        shared.update(_host_layer_inputs(inp, l))
    in_maps = []
    for c in range(NCORES):
        b, seg = divmod(c, 4)
        s0 = seg * TOK
        xc = np.zeros((cxh + TOK, D), dtype=np.float32)
        xc[cxh:] = xfull[b, s0:s0 + TOK]
        if seg > 0:
            xc[:cxh] = xfull[b, s0 - cxh:s0]
        m = dict(shared)
        m.update(_core_consts(seg))
        m["x_core"] = xc
        in_maps.append(m)
    res = run_bass_kernel_spmd(nc, in_maps, core_ids=list(range(NCORES)))
    out = np.zeros_like(xfull)
    for c in range(NCORES):
        b, seg = divmod(c, 4)
        out[b, seg * TOK:(seg + 1) * TOK] = res.results[c]["out_core"]
    return out


def kernel(**inputs):
    inp = {k: np.asarray(v) for k, v in inputs.items()}
    x = np.ascontiguousarray(inp["x"], dtype=np.float32)
    if FUSED:
        return _run([0, 1], HALO, x, inp)
    x1 = _run([0], CX, x, inp)
    return _run([1], CX, x1, inp)
```

```python
import numpy as np
from contextlib import ExitStack
import concourse.bass as bass
import concourse.mybir as mybir
from concourse.bass_utils import run_bass_kernel_spmd

F32 = mybir.dt.float32
BF16 = mybir.dt.bfloat16
AF = mybir.ActivationFunctionType
ALU = mybir.AluOpType

D = 2048
DIN = 3584
DFF = 8192
NL = 2
ALPHA = float((2 * NL) ** 0.25)
EPS = 1e-5
NCORES = 8
TOK = 2048
G = 512
CX = 32
HALO = 160
SW = 256
NSLOT = 4
SPLIT = 8
WINS = (2, 4, 8, 16)

C_BIN = 0
C_PSC = 28
C_CW = 32
C_CB = 218
C_CLG = 224
C_CLB = 230
C_BOUT = 236
C_L1G = 252
C_L1B = 268
C_BF1 = 284
C_BF2 = 348
C_L2G = 364
C_L2B = 380
NPC = 396

FUSED = True


class Sched:
    def __init__(self, nc, es):
        self.nc = nc
        self.es = es
        self.E = {"pe": nc.tensor, "act": nc.scalar, "dve": nc.vector, "pool": nc.gpsimd, "sp": nc.sync}
        self.sem = {k: es.enter_context(nc.semaphore("s_" + k)) for k in self.E}
        self.cnt = {k: 0 for k in self.E}
        self.seen = {k: {} for k in self.E}
        self.lastw = {}
        self.readers = {}
        self.dsem = {}
        self.pe_open = False
        self.pe_last = None

    def _semh(self, k):
        return self.sem[k] if k in self.sem else self.dsem[k][0]

    def _wait(self, eng, ev):
        k, v = ev
        if self.seen[eng].get(k, 0) >= v:
            return
        self.seen[eng][k] = v
        self.E[eng].wait_ge(self._semh(k), v)

    def deps(self, eng, reads, writes):
        for k in reads:
            ev = self.lastw.get(k)
            if ev is not None and not (eng == "pe" and ev[0] == "pe"):
                self._wait(eng, ev)
        for k in writes:
            ev = self.lastw.get(k)
            if ev is not None and not (eng == "pe" and ev[0] == "pe"):
                self._wait(eng, ev)
            for sk, v in self.readers.get(k, {}).items():
                if eng == "pe" and sk == "pe":
                    continue
                self._wait(eng, (sk, v))

    def record(self, ev, reads, writes):
        for k in reads:
            r = self.readers.setdefault(k, {})
            if r.get(ev[0], 0) < ev[1]:
                r[ev[0]] = ev[1]
        for k in writes:
            self.lastw[k] = ev
            self.readers[k] = {}

    def op(self, eng, reads, writes, fn):
        assert not (eng == "pe")
        self.deps(eng, reads, writes)
        ins = fn(self.E[eng])
        self.cnt[eng] += 1
        ins.then_inc(self.sem[eng], 1)
        ev = (eng, self.cnt[eng])
        self.record(ev, reads, writes)
        return ev

    def mm(self, out, lhsT, rhs, reads, writes, start=True, stop=True):
        self.deps("pe", reads, writes)
        ins = self.nc.tensor.matmul(out, lhsT, rhs, start=start, stop=stop)
        ev = ("pe", self.cnt["pe"] + 1)
        self.record(ev, reads, writes)
        self.pe_last = ins
        self.pe_open = True

    def pe_end(self):
        assert self.pe_open
        self.cnt["pe"] += 1
        self.pe_last.then_inc(self.sem["pe"], 1)
        self.pe_open = False

    def new_dsem(self, name):
        self.dsem[name] = [self.es.enter_context(self.nc.semaphore("d_" + name)), 0]

    def dma(self, eng, semname, out, in_, reads, writes):
        assert not self.pe_open or eng != "pe"
        self.deps(eng, reads, writes)
        ins = self.E[eng].dma_start(out=out, in_=in_)
        s = self.dsem[semname]
        s[1] += 16
        ins.then_inc(s[0], 16)
        ev = (semname, s[1])
        self.record(ev, reads, writes)
        return ev


def build_program(layers, cxh):
    fused = len(layers) == 2
    nc = bass.Bass("TRN2", target_bir_lowering=False)
    es = ExitStack()
    S = Sched(nc, es)

    def dram(name, shape, kind="ExternalInput"):
        return nc.dram_tensor(name, list(shape), F32, kind=kind).ap()

    x_in = dram("x_core", [cxh + TOK, D])
    out_d = dram("out_core", [TOK, D], kind="ExternalOutput")
    W = {}
    for l in layers:
        W[l] = dict(
            w_in=dram(f"w_in{l}", [D, DIN]), w_out=dram(f"w_out{l}", [D, D]),
            w_ff1=dram(f"w_ff1{l}", [D, DFF]), w_ff2=dram(f"w_ff2{l}", [DFF, D]),
            pcols=dram(f"pcols{l}", [128, NPC]), rows=dram(f"rows{l}", [128, 3, 768]),
            bsb=dram(f"bsb{l}", [128, 6, 128]), wT=dram(f"wT{l}", [128, 6, 128]),
            wpool=dram(f"wpool{l}", [128, 4, 128]),
        )
    ident_d = dram("ident", [128, 128])
    maskT_d = dram("maskT", [128, 128])
    rc16_d = dram("rc16", [128, 4, 16])
    cmask_d = dram("cmask", [128, 1])

    def sb(name, shape, dt=F32):
        return es.enter_context(nc.sbuf_tensor("sb_" + name, list(shape), dt))

    xres = sb("xres", [128, 16, G])
    xT = sb("xT", [128, 16, G], BF16)
    mT = sb("mT", [128, 16, G], BF16)
    slabs = [sb(f"slab{i}", [128, 16, SW], BF16) for i in range(NSLOT)]
    stage = [sb(f"stage{i}", [128, 1024]) for i in range(2)]
    pa = sb("pa", [128, 4, CX + G])
    pooled = [sb(f"pooled{i}", [128, G], BF16) for i in range(2)]
    sA = sb("sA", [128, CX + G])
    sB = sb("sB", [128, CX + G])
    ub = sb("ub", [128, 6, G])
    vraw = sb("vraw", [128, 4, 768])
    vnb = [sb(f"vnb{i}", [128, 768], BF16) for i in range(2)]
    hglu = sb("hglu", [128, 6, CX + G])
    sig = [sb(f"sig{i}", [128, G]) for i in range(2)]
    acc = sb("acc", [128, 6, G])
    zb = sb("zb", [128, G], BF16)
    zsq = sb("zsq", [128, G], BF16)
    stA = sb("stA", [128, G])
    stB = sb("stB", [128, G])
    stC = sb("stC", [128, G])
    SD = nc.vector.BN_STATS_DIM
    bnst = sb("bnst", [128, 2, SD])
    bnag = sb("bnag", [128, 4])
    pactx = {l: sb(f"pactx{l}", [128, 4, CX]) for l in layers}
    hctx = {l: sb(f"hctx{l}", [128, 6, CX]) for l in layers}
    pcols = {l: sb(f"pcols{l}", [128, NPC]) for l in layers}
    pca = {l: sb(f"pca{l}", [128, 64]) for l in layers}
    rows = sb("rows", [128, 3, 768])
    bsb = sb("bsb", [128, 6, 128])
    wTraw = sb("wTraw", [128, 6, 128], BF16)
    wTm = sb("wTm", [128, 6, 128], BF16)
    wpool = sb("wpool", [128, 4, 128], BF16)
    ident = sb("ident", [128, 128])
    maskT = sb("maskT", [128, 128], BF16)
    ones = sb("ones", [128, 128], BF16)
    rc16 = sb("rc16", [128, 4, 16])
    cmask = sb("cmask", [128, 1])
    epsc = sb("epsc", [128, 1])
    zc = sb("zc", [128, 1])
    ps = [es.enter_context(nc.psum_tensor(f"ps{i}", [128, 512], F32)) for i in range(8)]
    print("sbuf bytes remaining/partition:", nc.sbuf_bytes_remaining)

    for i in range(NSLOT):
        S.new_dsem(f"slab{i}")
    for n in ["setup", "setup_p", "mixc", "mixc_p", "st0", "st1", "out0", "out1"]:
        S.new_dsem(n)

    bank_ctr = [0]

    def bank():
        b = bank_ctr[0] % 8
        bank_ctr[0] += 1
        return b

    units = []
    if fused:
        units.append(dict(kind="halo", row0=0, N=128, cx=CX))
    else:
        units.append(dict(kind="ctx", row0=0, N=CX, cx=0))
    for g in range(TOK // G):
        units.append(dict(kind="main", row0=cxh + g * G, N=G, cx=0, g=g))

    INP_ORDER = [0, 2, 16, 18, 20, 22, 24, 26, 4, 6, 8, 10, 12, 14]
    CTX_ORDER = [0, 2, 16, 18, 20, 22, 24, 26]

    def layer_slabs(l, ctx_only):
        lst = []
        wv = W[l]["w_in"].rearrange("(kc p) c -> p kc c", p=128)
        for c0 in (CTX_ORDER if ctx_only else INP_ORDER):
            lst.append(("in", l, c0, wv[:, :, c0 * 128:c0 * 128 + SW]))
        if ctx_only:
            return lst
        wv = W[l]["w_out"].rearrange("(kc p) c -> p kc c", p=128)
        for c0 in range(0, 16, 2):
            lst.append(("out", l, c0, wv[:, :, c0 * 128:c0 * 128 + SW]))
        w1 = W[l]["w_ff1"].rearrange("(kc p) c -> p kc c", p=128)
        w2 = W[l]["w_ff2"].rearrange("(kc p) c -> p kc c", p=128)
        for q in range(4):
            for c0 in range(0, 16, 2):
                f0 = q * 16 + c0
                lst.append(("ff1", l, f0, w1[:, :, f0 * 128:f0 * 128 + SW]))
            for c0 in range(0, 16, 2):
                lst.append(("ff2", l, (q, c0), w2[:, q * 16:(q + 1) * 16, c0 * 128:c0 * 128 + SW]))
        return lst

    slab_list = []
    for u in units:
        if u["kind"] == "halo":
            slab_list += layer_slabs(layers[0], False) + layer_slabs(layers[1], True)
        elif u["kind"] == "ctx":
            slab_list += layer_slabs(layers[0], True)
        else:
            for l in layers:
                slab_list += layer_slabs(l, False)
    slab_state = dict(next=0, issued=0)

    def issue_slabs(upto):
        while slab_state["issued"] <= upto and slab_state["issued"] < len(slab_list):
            j = slab_state["issued"]
            slot = j % NSLOT
            key = ("slab", slot)
            S.deps("pool", [], [key])
            sem = S.dsem[f"slab{slot}"]
            for k2 in range(0, 16, SPLIT):
                ins = nc.gpsimd.dma_start(out=slabs[slot][:, k2:k2 + SPLIT, :], in_=slab_list[j][3][:, k2:k2 + SPLIT, :])
                sem[1] += 16
                ins.then_inc(sem[0], 16)
            S.record((f"slab{slot}", sem[1]), [], [key])
            slab_state["issued"] += 1

    def get_slab(kind, l, tag):
        i = slab_state["next"]
        d = slab_list[i]
        assert d[0] == kind and d[1] == l and d[2] == tag, (d[:3], kind, l, tag)
        slab_state["next"] += 1
        issue_slabs(i + NSLOT - 1)
        return i % NSLOT

    setup_keys = []

    def setup_dma(eng, out, in_, key):
        sn = "setup" if eng == "sp" else "setup_p"
        S.dma(eng, sn, out, in_, [], [key])
        setup_keys.append((key, sn))

    setup_dma("sp", ident[:], ident_d, "ident")
    setup_dma("sp", rc16[:], rc16_d, "rc16")
    setup_dma("sp", cmask[:], cmask_d, "cmask")
    for l in layers:
        setup_dma("sp", pcols[l][:], W[l]["pcols"], ("pcols", l))
    setup_dma("pool", maskT[:], maskT_d, "maskT")
    for k, sn in setup_keys:
        S.lastw[k] = (sn, S.dsem[sn][1])
    S.op("dve", [], ["ones"], lambda e: e.memset(ones[:], 1.0))
    S.op("dve", [], ["epsc"], lambda e: e.memset(epsc[:], EPS))
    S.op("dve", [], ["zc"], lambda e: e.memset(zc[:], 0.0))
    for l in layers:
        for (src, dst) in ((C_L1G, 0), (C_L1B, 16), (C_L2G, 32), (C_L2B, 48)):
            S.op("dve", [("pcols", l)], [("pca", l)],
                 lambda e, l=l, src=src, dst=dst: e.tensor_scalar(
                     out=pca[l][:, dst:dst + 16], in0=pcols[l][:, src:src + 16],
                     scalar1=ALPHA, scalar2=None, op0=ALU.mult))
    issue_slabs(NSLOT - 1)

    mixc_state = dict(layer=None)

    def load_mix_consts(l):
        if mixc_state["layer"] == l:
            return
        mixc_state["layer"] = l
        S.dma("sp", "mixc", rows[:], W[l]["rows"], [], ["rows"])
        S.dma("sp", "mixc", bsb[:], W[l]["bsb"], [], ["bsb"])
        S.dma("pool", "mixc_p", wTraw[:], W[l]["wT"], [], ["wTraw"])
        S.dma("pool", "mixc_p", wpool[:], W[l]["wpool"], [], ["wpool"])
        for k in ("rows", "bsb"):
            S.lastw[k] = ("mixc", S.dsem["mixc"][1])
        for k in ("wTraw", "wpool"):
            S.lastw[k] = ("mixc_p", S.dsem["mixc_p"][1])
        S.op("dve", ["wTraw", "maskT"], ["wTm"],
             lambda e: e.tensor_tensor(out=wTm[:], in0=wTraw[:],
                                       in1=maskT[:].unsqueeze(1).broadcast_to([128, 6, 128]), op=ALU.mult))

    def pc(l, col):
        return pcols[l][:, col:col + 1]

    def ln_fm(srcs, Dn, N, out_fn, ring=False):
        bs_, bq_ = bank(), bank()
        n = len(srcs)
        if ring:
            zbr = [(("xT", j), xT[:, j, 0:N]) for j in range(8)]
            zqr = [(("xT", 8 + j), xT[:, 8 + j, 0:N]) for j in range(8)]
        else:
            zbr = [("zb", zb[:, 0:N])]
            zqr = [("zsq", zsq[:, 0:N])]
        depth = len(zbr)

        def conv_ops(i):
            k, ap = srcs[i]
            zk, zap = zbr[i % depth]
            qk, qap = zqr[i % depth]
            S.op("act", [k], [zk], lambda e: e.activation(out=zap, in_=ap, func=AF.Copy, bias=0.0, scale=1.0))
            S.op("act", [k], [qk], lambda e: e.activation(out=qap, in_=ap, func=AF.Square, bias=0.0, scale=1.0))

        def mm_ops(i):
            zk, zap = zbr[i % depth]
            qk, qap = zqr[i % depth]
            S.mm(ps[bs_][:, 0:N], ones[:], zap, [zk, "ones"], [("ps", bs_)], start=(i == 0), stop=(i == n - 1))
            S.pe_end()
            S.mm(ps[bq_][:, 0:N], ones[:], qap, [qk, "ones"], [("ps", bq_)], start=(i == 0), stop=(i == n - 1))
            S.pe_end()

        for i in range(min(depth, n)):
            conv_ops(i)
        for i in range(n):
            mm_ops(i)
            if i + depth < n:
                conv_ops(i + depth)
        inv = 1.0 / Dn
        S.op("dve", [("ps", bs_)], ["stA"], lambda e: e.tensor_scalar(out=stA[:, 0:N], in0=ps[bs_][:, 0:N], scalar1=inv, scalar2=None, op0=ALU.mult))
        S.op("dve", [("ps", bq_)], ["stB"], lambda e: e.tensor_scalar(out=stB[:, 0:N], in0=ps[bq_][:, 0:N], scalar1=inv, scalar2=None, op0=ALU.mult))
        S.op("dve", ["stA"], ["stC"], lambda e: e.tensor_tensor(out=stC[:, 0:N], in0=stA[:, 0:N], in1=stA[:, 0:N], op=ALU.mult))
        S.op("dve", ["stB", "stC"], ["stB"], lambda e: e.tensor_tensor(out=stB[:, 0:N], in0=stB[:, 0:N], in1=stC[:, 0:N], op=ALU.subtract))
        S.op("act", ["stB", "epsc"], ["stB"], lambda e: e.activation(out=stB[:, 0:N], in_=stB[:, 0:N], func=AF.Sqrt, bias=epsc[:, 0:1], scale=1.0))
        S.op("dve", ["stB"], ["stB"], lambda e: e.reciprocal(out=stB[:, 0:N], in_=stB[:, 0:N]))
        S.op("dve", ["stA", "stB"], ["stA"], lambda e: e.scalar_tensor_tensor(out=stA[:, 0:N], in0=stA[:, 0:N], scalar=-1.0, in1=stB[:, 0:N], op0=ALU.mult, op1=ALU.mult))
        for i, (k, ap) in enumerate(srcs):
            S.op("dve", [k, "stB"], [k], lambda e, ap=ap: e.tensor_tensor(out=ap, in0=ap, in1=stB[:, 0:N], op=ALU.mult))
        for i, (k, ap) in enumerate(srcs):
            S.op("dve", [k, "stA"], [k], lambda e, ap=ap: e.tensor_tensor(out=ap, in0=ap, in1=stA[:, 0:N], op=ALU.add))
        for i in range(n):
            out_fn(i)

    def load_x(u):
        ntok = u["cx"] + u["N"]
        t0 = 0
        while t0 < ntok:
            tn = min(128, ntok - t0)
            for h in range(2):
                S.dma("sp", f"st{h}", stage[h][0:tn, :], x_in[u["row0"] + t0:u["row0"] + t0 + tn, h * 1024:(h + 1) * 1024],
                      [], [("stage", h)])
                for b4 in range(2):
                    b = bank()
                    c0 = h * 8 + b4 * 4
                    for j in range(4):
                        S.mm(ps[b][:, j * 128:j * 128 + tn], stage[h][0:tn, (b4 * 4 + j) * 128:(b4 * 4 + j + 1) * 128],
                             ident[0:tn, 0:tn], [("stage", h), "ident"], [("ps", b)])
                    S.pe_end()
                    pv = ps[b][:].rearrange("p (j t) -> p j t", j=4)[:, :, 0:tn]
                    ck = [("xres", c0 + j) for j in range(4)]
                    tk = [("xT", c0 + j) for j in range(4)]
                    S.op("act", [("ps", b)], tk, lambda e, pv=pv, c0=c0, t0=t0, tn=tn: e.activation(
                        out=xT[:, c0:c0 + 4, t0:t0 + tn], in_=pv, func=AF.Copy, bias=0.0, scale=1.0))
                    lo = max(t0, u["cx"])
                    if lo < t0 + tn:
                        pvr = ps[b][:].rearrange("p (j t) -> p j t", j=4)[:, :, lo - t0:tn]
                        S.op("dve", [("ps", b)] + tk, ck, lambda e, pvr=pvr, c0=c0, lo=lo, t0=t0, tn=tn: e.tensor_scalar(
                            out=xres[:, c0:c0 + 4, lo - u["cx"]:t0 + tn - u["cx"]], in0=pvr, scalar1=ALPHA, scalar2=None, op0=ALU.mult))
            t0 += tn

    def store_out(u):
        N = u["N"]
        orow = u["row0"] - cxh
        for tt in range(N // 128):
            for h in range(2):
                for b4 in range(2):
                    b = bank()
                    c0 = h * 8 + b4 * 4
                    for j in range(4):
                        S.mm(ps[b][:, j * 128:(j + 1) * 128], xres[:, c0 + j, tt * 128:(tt + 1) * 128], ident[:],
                             [("xres", c0 + j), "ident"], [("ps", b)])
                    S.pe_end()
                    S.op("act", [("ps", b)], [("stage", h)], lambda e, b=b, h=h, b4=b4: e.activation(
                        out=stage[h][:, b4 * 512:(b4 + 1) * 512], in_=ps[b][:], func=AF.Copy, bias=0.0, scale=1.0))
                S.dma("sp", f"out{h}", out_d[orow + tt * 128:orow + (tt + 1) * 128, h * 1024:(h + 1) * 1024], stage[h][:],
                      [("stage", h)], [("outd", tt, h, u["g"])])

    def proj_batch(slot, j, src, N, col0, nk=16):
        b = bank()
        for kc in range(nk):
            S.mm(ps[b][:, 0:N], slabs[slot][:, kc, j * 128:(j + 1) * 128], src[:, kc, col0:col0 + N],
                 [("slab", slot), (src_key[id(src)], kc)], [("ps", b)], start=(kc == 0), stop=(kc == nk - 1))
        S.pe_end()
        return b

    src_key = {id(xT): "xT", id(mT): "mT"}

    def mix_inproj(l, u, ctx_only):
        N, cx = u["N"], u["cx"]
        if ctx_only:
            ncol, xcol0, dcol0 = CX, u["xcol0"], CX
        else:
            ncol, xcol0, dcol0 = cx + N, 0, CX - cx
        if not ctx_only:
            load_mix_consts(l)
            if cx == 0:
                S.op("dve", [("pactx", l)], [("pa", i) for i in range(4)],
                     lambda e: e.tensor_copy(out=pa[:, :, 0:CX], in_=pactx[l][:]))
                S.op("dve", [("hctx", l)], [("hglu", i) for i in range(6)],
                     lambda e: e.tensor_copy(out=hglu[:, :, 0:CX], in_=hctx[l][:]))
        order = CTX_ORDER if ctx_only else INP_ORDER
        for c0 in order:
            slot = get_slab("in", l, c0)
            if 10 <= c0 < 16:
                vc0 = (c0 - 10) * 128
                for tt in range(N // 128):
                    b = bank()
                    for kc in range(16):
                        S.mm(ps[b][:, 0:SW], xT[:, kc, cx + tt * 128:cx + (tt + 1) * 128], slabs[slot][:, kc, :],
                             [("slab", slot), ("xT", kc)], [("ps", b)], start=(kc == 0), stop=(kc == 15))
                    S.pe_end()
                    S.op("act", [("ps", b)], [("vraw", tt)], lambda e, b=b, tt=tt, vc0=vc0: e.activation(
                        out=vraw[:, tt, vc0:vc0 + SW], in_=ps[b][:, 0:SW], func=AF.Copy, bias=0.0, scale=1.0))
                if c0 == 14:
                    sgu_path(l, u)
                    conv_ln(l, u)
                continue
            for j in range(2):
                c = c0 + j
                if c < 4:
                    b = proj_batch(slot, j, xT, ncol, xcol0)
                    S.op("act", [("ps", b), ("pcols", l)], [("pa", c)], lambda e, b=b, c=c: e.activation(
                        out=pa[:, c, dcol0:dcol0 + ncol], in_=ps[b][:, 0:ncol], func=AF.Identity, bias=pc(l, C_BIN + c), scale=1.0))
                elif c < 10:
                    i = c - 4
                    b = proj_batch(slot, j, xT, N, cx)
                    S.op("act", [("ps", b), ("pcols", l)], [("ub", i)], lambda e, b=b, c=c, i=i: e.activation(
                        out=ub[:, i, 0:N], in_=ps[b][:, 0:N], func=AF.Gelu_apprx_tanh, bias=pc(l, C_BIN + c), scale=1.0))
                elif c < 22:
                    i = c - 16
                    b = proj_batch(slot, j, xT, ncol, xcol0)
                    S.op("act", [("ps", b), ("pcols", l)], [("hglu", i)], lambda e, b=b, c=c, i=i: e.activation(
                        out=hglu[:, i, dcol0:dcol0 + ncol], in_=ps[b][:, 0:ncol], func=AF.Identity, bias=pc(l, C_BIN + c), scale=1.0))
                else:
                    i = c - 22
                    b = proj_batch(slot, j, xT, ncol, xcol0)
                    sg = sig[i % 2]
                    sk = ("sig", i % 2)
                    S.op("act", [("ps", b), ("pcols", l)], [sk], lambda e, b=b, c=c, sg=sg: e.activation(
                        out=sg[:, 0:ncol], in_=ps[b][:, 0:ncol], func=AF.Sigmoid, bias=pc(l, C_BIN + c), scale=1.0))
                    S.op("dve", [sk, ("hglu", i)], [("hglu", i)], lambda e, i=i, sg=sg: e.tensor_tensor(
                        out=hglu[:, i, dcol0:dcol0 + ncol], in0=hglu[:, i, dcol0:dcol0 + ncol], in1=sg[:, 0:ncol], op=ALU.mult))
            if c0 == 2:
                if ctx_only:
                    save_ctx_pa(l, CX, True)
                else:
                    pool_path(l, u)
            if c0 == 20 and not ctx_only:
                pool_pe(l, u)
            if c0 == 26:
                if ctx_only:
                    save_ctx_h(l, CX, True)
                else:
                    conv_path(l, u)

    def save_ctx_pa(l, N, masked):
        rk = [("pa", i) for i in range(4)]
        if masked:
            S.op("dve", rk + ["cmask"], [("pactx", l)], lambda e: e.tensor_scalar(
                out=pactx[l][:], in0=pa[:, :, N:N + CX], scalar1=cmask[:, 0:1], scalar2=None, op0=ALU.mult))
        else:
            S.op("dve", rk, [("pactx", l)], lambda e: e.tensor_copy(out=pactx[l][:], in_=pa[:, :, N:N + CX]))

    def save_ctx_h(l, N, masked):
        rk = [("hglu", i) for i in range(6)]
        if masked:
            S.op("dve", rk + ["cmask"], [("hctx", l)], lambda e: e.tensor_scalar(
                out=hctx[l][:], in0=hglu[:, :, N:N + CX], scalar1=cmask[:, 0:1], scalar2=None, op0=ALU.mult))
        else:
            S.op("dve", rk, [("hctx", l)], lambda e: e.tensor_copy(out=hctx[l][:], in_=hglu[:, :, N:N + CX]))

    pbufs = [(pooled[0], ("pooled", 0)), (pooled[1], ("pooled", 1)), (zb, "zb"), (zsq, "zsq")]

    def pool_path(l, u):
        N = u["N"]
        first_main = (u["kind"] == "main" and u["g"] == 0)
        E = CX + N
        for gi, win in enumerate(WINS):
            a = pa[:, gi, :]
            ak = ("pa", gi)
            bufs = [(sA, "sA"), (sB, "sB")]
            src, srck = a, ak
            lo = CX - (win - 1)
            step = 1
            bi = 0
            while step < win:
                dst, dstk = bufs[bi]
                lo2 = lo + step
                S.op("dve", [srck], [dstk], lambda e, dst=dst, src=src, lo2=lo2, step=step: e.tensor_tensor(
                    out=dst[:, lo2:E], in0=src[:, lo2:E], in1=src[:, lo2 - step:E - step], op=ALU.add))
                src, srck = dst, dstk
                lo = lo2
                step *= 2
                bi ^= 1
            pb, pk = pbufs[gi]
            S.op("dve", [srck, ak], [pk], lambda e, pb=pb, src=src, a=a, win=win: e.scalar_tensor_tensor(
                out=pb[:, 0:N], in0=src[:, CX:E], scalar=1.0 / win, in1=a[:, CX:E], op0=ALU.mult, op1=ALU.subtract))
            if first_main:
                S.op("dve", [srck, "rc16"], ["stC"], lambda e, src=src, gi=gi: e.tensor_tensor(
                    out=stC[:, 0:16], in0=src[:, CX:CX + 16], in1=rc16[:, gi, :], op=ALU.mult))
                S.op("dve", ["stC", ak], [pk], lambda e, pb=pb, a=a: e.tensor_tensor(
                    out=pb[:, 0:16], in0=stC[:, 0:16], in1=a[:, CX:CX + 16], op=ALU.subtract))
        save_ctx_pa(l, N, u["kind"] == "halo")

    def pool_pe(l, u):
        N = u["N"]
        for gi in range(4):
            pb, pk = pbufs[gi]
            b = bank()
            S.mm(ps[b][:, 0:N], wpool[:, gi, :], pb[:, 0:N], ["wpool", pk], [("ps", b)])
            S.pe_end()
            S.op("act", [("ps", b), ("pcols", l), "zc"], [("mT", gi)], lambda e, b=b, gi=gi: e.activation(
                out=mT[:, gi, 0:N], in_=ps[b][:, 0:N], func=AF.Identity, bias=zc[:, 0:1], scale=pc(l, C_PSC + gi)))

    def conv_path(l, u):
        N = u["N"]
        for k in range(31):
            for i in range(6):
                wcol = pcols[l][:, C_CW + i * 31 + k:C_CW + i * 31 + k + 1]
                if k == 0:
                    S.op("dve", [("hglu", i), ("pcols", l)], [("acc", i)], lambda e, i=i, wcol=wcol: e.tensor_scalar(
                        out=acc[:, i, 0:N], in0=hglu[:, i, 2:2 + N], scalar1=wcol, scalar2=pc(l, C_CB + i),
                        op0=ALU.mult, op1=ALU.add))
                else:
                    S.op("dve", [("hglu", i), ("pcols", l), ("acc", i)], [("acc", i)], lambda e, i=i, wcol=wcol, k=k: e.scalar_tensor_tensor(
                        out=acc[:, i, 0:N], in0=hglu[:, i, 2 + k:2 + k + N], scalar=wcol, in1=acc[:, i, 0:N],
                        op0=ALU.mult, op1=ALU.add))
        save_ctx_h(l, N, u["kind"] == "halo")

    def conv_ln(l, u):
        N = u["N"]

        def outf(i):
            S.op("act", [("acc", i), ("pcols", l)], [("mT", 10 + i)], lambda e, i=i: e.activation(
                out=mT[:, 10 + i, 0:N], in_=acc[:, i, 0:N], func=AF.Silu, bias=pc(l, C_CLB + i), scale=pc(l, C_CLG + i)))
        ln_fm([(("acc", i), acc[:, i, 0:N]) for i in range(6)], 768.0, N, outf)

    def sgu_path(l, u):
        N = u["N"]
        ntt = N // 128
        for tt in range(ntt):
            vk = ("vraw", tt)
            v = vraw[:, tt, :]
            S.op("dve", [vk, "rows"], [vk], lambda e, v=v: e.tensor_tensor(out=v, in0=v, in1=rows[:, 0, :], op=ALU.add))
            S.op("act", [vk], [vk], lambda e, v=v: e.activation(out=v, in_=v, func=AF.Gelu_apprx_tanh, bias=0.0, scale=1.0))
            S.op("dve", [vk], ["bnst"], lambda e, v=v: e.bn_stats(out=bnst[:, 0, :], in_=v[:, 0:384]))
            S.op("dve", [vk], ["bnst"], lambda e, v=v: e.bn_stats(out=bnst[:, 1, :], in_=v[:, 384:768]))
            S.op("dve", ["bnst"], ["bnag"], lambda e: e.bn_aggr(out=bnag[:, 0:2], in_=bnst[:]))
            S.op("act", ["bnag", "epsc"], ["bnag2"], lambda e: e.activation(out=bnag[:, 2:3], in_=bnag[:, 1:2], func=AF.Sqrt, bias=epsc[:, 0:1], scale=1.0))
            S.op("dve", ["bnag2"], ["bnag3"], lambda e: e.reciprocal(out=bnag[:, 3:4], in_=bnag[:, 2:3]))
            S.op("dve", [vk, "bnag", "bnag3"], [vk], lambda e, v=v: e.tensor_scalar(
                out=v, in0=v, scalar1=bnag[:, 0:1], scalar2=bnag[:, 3:4], op0=ALU.subtract, op1=ALU.mult))
            S.op("dve", [vk, "rows"], [vk], lambda e, v=v: e.tensor_tensor(out=v, in0=v, in1=rows[:, 1, :], op=ALU.mult))
            vb = vnb[tt % 2]
            vbk = ("vnb", tt % 2)
            S.op("dve", [vk, "rows"], [vbk], lambda e, v=v, vb=vb: e.tensor_tensor(out=vb[:], in0=v, in1=rows[:, 2, :], op=ALU.add))
            for h in range(6):
                b = bank()
                S.mm(ps[b][:, 0:128], vb[:, h * 128:(h + 1) * 128], wTm[:, h, :], [vbk, "wTm"], [("ps", b)])
                S.pe_end()
                S.op("dve", [("ps", b), "bsb"], ["stC"], lambda e, b=b, h=h: e.tensor_tensor(
                    out=stC[:, 0:128], in0=ps[b][:, 0:128], in1=bsb[:, h, :], op=ALU.add))
                S.op("dve", ["stC", ("ub", h)], [("mT", 4 + h)], lambda e, h=h, tt=tt: e.tensor_tensor(
                    out=mT[:, 4 + h, tt * 128:(tt + 1) * 128], in0=stC[:, 0:128], in1=ub[:, h, tt * 128:(tt + 1) * 128], op=ALU.mult))

    def out_proj_ln1(l, u):
        N = u["N"]
        for c0 in range(0, 16, 2):
            slot = get_slab("out", l, c0)
            for j in range(2):
                c = c0 + j
                b = proj_batch(slot, j, mT, N, 0)
                S.op("dve", [("ps", b), ("pcols", l), ("xres", c)], [("xres", c)], lambda e, b=b, c=c: e.scalar_tensor_tensor(
                    out=xres[:, c, 0:N], in0=ps[b][:, 0:N], scalar=pc(l, C_BOUT + c), in1=xres[:, c, 0:N], op0=ALU.add, op1=ALU.add))

        def outf(i):
            S.op("act", [("xres", i), ("pcols", l)], [("xT", i)], lambda e, i=i: e.activation(
                out=xT[:, i, 0:N], in_=xres[:, i, 0:N], func=AF.Identity, bias=pc(l, C_L1B + i), scale=pc(l, C_L1G + i)))
            S.op("dve", [("xres", i), ("pca", l)], [("xres", i)], lambda e, i=i: e.tensor_scalar(
                out=xres[:, i, 0:N], in0=xres[:, i, 0:N], scalar1=pca[l][:, i:i + 1], scalar2=pca[l][:, 16 + i:17 + i],
                op0=ALU.mult, op1=ALU.add))
        ln_fm([(("xres", i), xres[:, i, 0:N]) for i in range(16)], float(D), N, outf, ring=True)

    def ffn_ln2(l, u, final, need_res):
        N = u["N"]
        for q in range(4):
            for c0 in range(0, 16, 2):
                f0 = q * 16 + c0
                slot = get_slab("ff1", l, f0)
                for j in range(2):
                    f = f0 + j
                    b = proj_batch(slot, j, xT, N, 0)
                    sg = sig[f % 2]
                    sk = ("sig", f % 2)
                    S.op("act", [("ps", b), ("pcols", l)], [sk], lambda e, b=b, f=f, sg=sg: e.activation(
                        out=sg[:, 0:N], in_=ps[b][:, 0:N], func=AF.Relu, bias=pc(l, C_BF1 + f), scale=1.0))
                    S.op("dve", [sk], [("mT", c0 + j)], lambda e, sg=sg, c0=c0, j=j: e.tensor_tensor(
                        out=mT[:, c0 + j, 0:N], in0=sg[:, 0:N], in1=sg[:, 0:N], op=ALU.mult))
            for c0 in range(0, 16, 2):
                slot = get_slab("ff2", l, (q, c0))
                for j in range(2):
                    c = c0 + j
                    b = proj_batch(slot, j, mT, N, 0)
                    if q == 0:
                        S.op("dve", [("ps", b), ("pcols", l), ("xres", c)], [("xres", c)], lambda e, b=b, c=c: e.scalar_tensor_tensor(
                            out=xres[:, c, 0:N], in0=ps[b][:, 0:N], scalar=pc(l, C_BF2 + c), in1=xres[:, c, 0:N], op0=ALU.add, op1=ALU.add))
                    else:
                        S.op("dve", [("ps", b), ("xres", c)], [("xres", c)], lambda e, b=b, c=c: e.tensor_tensor(
                            out=xres[:, c, 0:N], in0=ps[b][:, 0:N], in1=xres[:, c, 0:N], op=ALU.add))

        def outf(i):
            if final:
                S.op("dve", [("xres", i), ("pcols", l)], [("xres", i)], lambda e, i=i: e.tensor_scalar(
                    out=xres[:, i, 0:N], in0=xres[:, i, 0:N], scalar1=pc(l, C_L2G + i), scalar2=pc(l, C_L2B + i),
                    op0=ALU.mult, op1=ALU.add))
            else:
                S.op("act", [("xres", i), ("pcols", l)], [("xT", i)], lambda e, i=i: e.activation(
                    out=xT[:, i, 0:N], in_=xres[:, i, 0:N], func=AF.Identity, bias=pc(l, C_L2B + i), scale=pc(l, C_L2G + i)))
                if need_res:
                    S.op("dve", [("xres", i), ("pca", l)], [("xres", i)], lambda e, i=i: e.tensor_scalar(
                        out=xres[:, i, 0:N], in0=xres[:, i, 0:N], scalar1=pca[l][:, 32 + i:33 + i], scalar2=pca[l][:, 48 + i:49 + i],
                        op0=ALU.mult, op1=ALU.add))
        ln_fm([(("xres", i), xres[:, i, 0:N]) for i in range(16)], float(D), N, outf, ring=True)

    import os
    KSTOP = int(os.environ.get("KSTOP", "0"))
    class StopEmit(Exception):
        pass
    stage_ctr = [0]
    def ckpt():
        stage_ctr[0] += 1
        if KSTOP and stage_ctr[0] >= KSTOP:
            raise StopEmit()
    try:
      for u in units:
        load_x(u)
        ckpt()
        if u["kind"] == "ctx":
            u2 = dict(u)
            u2["xcol0"] = 0
            mix_inproj(layers[0], u2, True)
            ckpt()
            continue
        if u["kind"] == "halo":
            l = layers[0]
            mix_inproj(l, u, False)
            out_proj_ln1(l, u)
            ffn_ln2(l, u, False, False)
            u2 = dict(u)
            u2["xcol0"] = u["N"] - CX
            mix_inproj(layers[1], u2, True)
            continue
        for li, l in enumerate(layers):
            last = (li == len(layers) - 1)
            mix_inproj(l, u, False)
            ckpt()
            out_proj_ln1(l, u)
            ckpt()
            ffn_ln2(l, u, last, True)
            ckpt()
        store_out(u)
        ckpt()
      assert slab_state["next"] == len(slab_list), (slab_state, len(slab_list))
    except StopEmit:
        if S.pe_open:
            S.pe_end()
        for k in ("pe", "act", "dve", "pool"):
            if S.cnt[k]:
                nc.sync.wait_ge(S.sem[k], S.cnt[k])
        for k, v in S.dsem.items():
            if v[1]:
                nc.sync.wait_ge(v[0], v[1])
    for h in range(2):
        nc.sync.wait_ge(S.dsem[f"out{h}"][0], S.dsem[f"out{h}"][1])
    es.close()
    return nc


def _host_layer_inputs(inp, l):
    f = np.float32
    cols = []
    cols.append(inp["b_in"][l].reshape(28, 128).T)
    cols.append(inp["pool_scale"][l].reshape(4, 128).T)
    cols.append(inp["conv_w"][l].T.reshape(6, 128, 31).transpose(1, 0, 2).reshape(128, 186))
    for n, k in (("conv_b", 6), ("conv_ln_g", 6), ("conv_ln_b", 6), ("b_out", 16), ("ln1_g", 16), ("ln1_b", 16),
                 ("b_ff1", 64), ("b_ff2", 16), ("ln2_g", 16), ("ln2_b", 16)):
        cols.append(inp[n][l].reshape(k, 128).T)
    pcols = np.ascontiguousarray(np.concatenate(cols, axis=1), dtype=f)
    assert pcols.shape == (128, NPC)
    rows = np.stack([inp["b_in"][l][1280:2048], inp["sgu_ln_g"][l], inp["sgu_ln_b"][l]], axis=0)
    rows = np.ascontiguousarray(np.broadcast_to(rows[None], (128, 3, 768)), dtype=f)
    bsb = np.ascontiguousarray(np.broadcast_to(inp["sgu_b"][l][None], (128, 6, 128)), dtype=f)
    wT = np.ascontiguousarray(inp["sgu_w"][l].transpose(2, 0, 1), dtype=f)
    wpool = np.ascontiguousarray(inp["w_pool"][l].transpose(1, 0, 2), dtype=f)
    return {
        f"w_in{l}": np.ascontiguousarray(inp["w_in"][l]), f"w_out{l}": np.ascontiguousarray(inp["w_out"][l]),
        f"w_ff1{l}": np.ascontiguousarray(inp["w_ff1"][l]), f"w_ff2{l}": np.ascontiguousarray(inp["w_ff2"][l]),
        f"pcols{l}": pcols, f"rows{l}": rows, f"bsb{l}": bsb, f"wT{l}": wT, f"wpool{l}": wpool,
    }


def _core_consts(seg):
    ident = np.eye(128, dtype=np.float32)
    maskT = np.triu(np.ones((128, 128), dtype=np.float32))
    rc = np.zeros((128, 4, 16), dtype=np.float32)
    for gi, w in enumerate(WINS):
        for j in range(16):
            rc[:, gi, j] = 1.0 / (min(j + 1, w) if seg == 0 else w)
    cm = np.full((128, 1), 0.0 if seg == 0 else 1.0, dtype=np.float32)
    return {"ident": ident, "maskT": maskT, "rc16": rc, "cmask": cm}


def _run(layers, cxh, xfull, inp):
    nc = build_program(layers, cxh)
    shared = {}
    for l in layers:
        shared.update(_host_layer_inputs(inp, l))
    in_maps = []
    for c in range(NCORES):
        b, seg = divmod(c, 4)
        s0 = seg * TOK
        xc = np.zeros((cxh + TOK, D), dtype=np.float32)
        xc[cxh:] = xfull[b, s0:s0 + TOK]
        if seg > 0:
            xc[:cxh] = xfull[b, s0 - cxh:s0]
        m = dict(shared)
        m.update(_core_consts(seg))
        m["x_core"] = xc
        in_maps.append(m)
    res = run_bass_kernel_spmd(nc, in_maps, core_ids=list(range(NCORES)))
    out = np.zeros_like(xfull)
    for c in range(NCORES):
        b, seg = divmod(c, 4)
        out[b, seg * TOK:(seg + 1) * TOK] = res.results[c]["out_core"]
    return out


def kernel(**inputs):
    inp = {k: np.asarray(v) for k, v in inputs.items()}
    x = np.ascontiguousarray(inp["x"], dtype=np.float32)
    if FUSED:
        return _run([0, 1], HALO, x, inp)
    x1 = _run([0], CX, x, inp)
    return _run([1], CX, x1, inp)
```

```python
import numpy as np
from contextlib import ExitStack
import concourse.bass as bass
import concourse.mybir as mybir
from concourse.bass_utils import run_bass_kernel_spmd

F32 = mybir.dt.float32
BF16 = mybir.dt.bfloat16
AF = mybir.ActivationFunctionType
ALU = mybir.AluOpType

D = 2048
DIN = 3584
DFF = 8192
NL = 2
ALPHA = float((2 * NL) ** 0.25)
EPS = 1e-5
NCORES = 8
TOK = 2048
G = 512
CX = 32
HALO = 160
SW = 256
NSLOT = 4
SPLIT = 8
WINS = (2, 4, 8, 16)

C_BIN = 0
C_PSC = 28
C_CW = 32
C_CB = 218
C_CLG = 224
C_CLB = 230
C_BOUT = 236
C_L1G = 252
C_L1B = 268
C_BF1 = 284
C_BF2 = 348
C_L2G = 364
C_L2B = 380
NPC = 396

FUSED = True


class Sched:
    def __init__(self, nc, es):
        self.nc = nc
        self.es = es
        self.E = {"pe": nc.tensor, "act": nc.scalar, "dve": nc.vector, "pool": nc.gpsimd, "sp": nc.sync}
        self.sem = {k: es.enter_context(nc.semaphore("s_" + k)) for k in self.E}
        self.cnt = {k: 0 for k in self.E}
        self.seen = {k: {} for k in self.E}
        self.lastw = {}
        self.readers = {}
        self.dsem = {}
        self.pe_open = False
        self.pe_last = None

    def _semh(self, k):
        return self.sem[k] if k in self.sem else self.dsem[k][0]

    def _wait(self, eng, ev):
        k, v = ev
        if self.seen[eng].get(k, 0) >= v:
            return
        self.seen[eng][k] = v
        self.E[eng].wait_ge(self._semh(k), v)

    def deps(self, eng, reads, writes):
        for k in reads:
            ev = self.lastw.get(k)
            if ev is not None and not (eng == "pe" and ev[0] == "pe"):
                self._wait(eng, ev)
        for k in writes:
            ev = self.lastw.get(k)
            if ev is not None and not (eng == "pe" and ev[0] == "pe"):
                self._wait(eng, ev)
            for sk, v in self.readers.get(k, {}).items():
                if eng == "pe" and sk == "pe":
                    continue
                self._wait(eng, (sk, v))

    def record(self, ev, reads, writes):
        for k in reads:
            r = self.readers.setdefault(k, {})
            if r.get(ev[0], 0) < ev[1]:
                r[ev[0]] = ev[1]
        for k in writes:
            self.lastw[k] = ev
            self.readers[k] = {}

    def op(self, eng, reads, writes, fn):
        assert not (eng == "pe")
        self.deps(eng, reads, writes)
        ins = fn(self.E[eng])
        self.cnt[eng] += 1
        ins.then_inc(self.sem[eng], 1)
        ev = (eng, self.cnt[eng])
        self.record(ev, reads, writes)
        return ev

    def mm(self, out, lhsT, rhs, reads, writes, start=True, stop=True):
        self.deps("pe", reads, writes)
        ins = self.nc.tensor.matmul(out, lhsT, rhs, start=start, stop=stop)
        ev = ("pe", self.cnt["pe"] + 1)
        self.record(ev, reads, writes)
        self.pe_last = ins
        self.pe_open = True

    def pe_end(self):
        assert self.pe_open
        self.cnt["pe"] += 1
        self.pe_last.then_inc(self.sem["pe"], 1)
        self.pe_open = False

    def new_dsem(self, name):
        self.dsem[name] = [self.es.enter_context(self.nc.semaphore("d_" + name)), 0]

    def dma(self, eng, semname, out, in_, reads, writes):
        assert not self.pe_open or eng != "pe"
        self.deps(eng, reads, writes)
        ins = self.E[eng].dma_start(out=out, in_=in_)
        s = self.dsem[semname]
        s[1] += 16
        ins.then_inc(s[0], 16)
        ev = (semname, s[1])
        self.record(ev, reads, writes)
        return ev


def build_program(layers, cxh):
    fused = len(layers) == 2
    nc = bass.Bass("TRN2", target_bir_lowering=False)
    es = ExitStack()
    S = Sched(nc, es)

    def dram(name, shape, kind="ExternalInput"):
        return nc.dram_tensor(name, list(shape), F32, kind=kind).ap()

    x_in = dram("x_core", [cxh + TOK, D])
    out_d = dram("out_core", [TOK, D], kind="ExternalOutput")
    W = {}
    for l in layers:
        W[l] = dict(
            w_in=dram(f"w_in{l}", [D, DIN]), w_out=dram(f"w_out{l}", [D, D]),
            w_ff1=dram(f"w_ff1{l}", [D, DFF]), w_ff2=dram(f"w_ff2{l}", [DFF, D]),
            pcols=dram(f"pcols{l}", [128, NPC]), rows=dram(f"rows{l}", [128, 3, 768]),
            bsb=dram(f"bsb{l}", [128, 6, 128]), wT=dram(f"wT{l}", [128, 6, 128]),
            wpool=dram(f"wpool{l}", [128, 4, 128]),
        )
    ident_d = dram("ident", [128, 128])
    maskT_d = dram("maskT", [128, 128])
    rc16_d = dram("rc16", [128, 4, 16])
    cmask_d = dram("cmask", [128, 1])

    def sb(name, shape, dt=F32):
        return es.enter_context(nc.sbuf_tensor("sb_" + name, list(shape), dt))

    xres = sb("xres", [128, 16, G])
    xT = sb("xT", [128, 16, G], BF16)
    mT = sb("mT", [128, 16, G], BF16)
    slabs = [sb(f"slab{i}", [128, 16, SW], BF16) for i in range(NSLOT)]
    stage = [sb(f"stage{i}", [128, 1024]) for i in range(2)]
    pa = sb("pa", [128, 4, CX + G])
    pooled = [sb(f"pooled{i}", [128, G], BF16) for i in range(2)]
    sA = sb("sA", [128, CX + G])
    sB = sb("sB", [128, CX + G])
    ub = sb("ub", [128, 6, G])
    vraw = sb("vraw", [128, 4, 768])
    vnb = [sb(f"vnb{i}", [128, 768], BF16) for i in range(2)]
    hglu = sb("hglu", [128, 6, CX + G])
    sig = [sb(f"sig{i}", [128, G]) for i in range(2)]
    acc = sb("acc", [128, 6, G])
    zb = sb("zb", [128, G], BF16)
    zsq = sb("zsq", [128, G], BF16)
    stA = sb("stA", [128, G])
    stB = sb("stB", [128, G])
    stC = sb("stC", [128, G])
    SD = nc.vector.BN_STATS_DIM
    bnst4 = sb("bnst4", [128, 4, 2, SD])
    bnag4 = sb("bnag4", [128, 4, 4])
    pactx = {l: sb(f"pactx{l}", [128, 4, CX]) for l in layers}
    hctx = {l: sb(f"hctx{l}", [128, 6, CX]) for l in layers}
    pcols = {l: sb(f"pcols{l}", [128, NPC]) for l in layers}
    pca = {l: sb(f"pca{l}", [128, 64]) for l in layers}
    rows = sb("rows", [128, 3, 768])
    bsb = sb("bsb", [128, 6, 128])
    wTraw = sb("wTraw", [128, 6, 128], BF16)
    wTm = sb("wTm", [128, 6, 128], BF16)
    wpool = sb("wpool", [128, 4, 128], BF16)
    ident = sb("ident", [128, 128])
    maskT = sb("maskT", [128, 128], BF16)
    ones = sb("ones", [128, 128], BF16)
    rc16 = sb("rc16", [128, 4, 16])
    cmask = sb("cmask", [128, 1])
    epsc = sb("epsc", [128, 1])
    zc = sb("zc", [128, 1])
    ps = [es.enter_context(nc.psum_tensor(f"ps{i}", [128, 512], F32)) for i in range(8)]
    print("sbuf bytes remaining/partition:", nc.sbuf_bytes_remaining)

    for i in range(NSLOT):
        S.new_dsem(f"slab{i}")
    for n in ["setup", "setup_p", "mixc", "mixc_p", "st0", "st1", "out0", "out1"]:
        S.new_dsem(n)

    bank_ctr = [0]

    def bank():
        b = bank_ctr[0] % 8
        bank_ctr[0] += 1
        return b

    units = []
    if fused:
        units.append(dict(kind="halo", row0=0, N=128, cx=CX))
    else:
        units.append(dict(kind="ctx", row0=0, N=CX, cx=0))
    for g in range(TOK // G):
        units.append(dict(kind="main", row0=cxh + g * G, N=G, cx=0, g=g))

    INP_ORDER = [0, 2, 16, 18, 20, 22, 24, 26, 4, 6, 8, 10, 12, 14]
    CTX_ORDER = [0, 2, 16, 18, 20, 22, 24, 26]

    def layer_slabs(l, ctx_only):
        lst = []
        wv = W[l]["w_in"].rearrange("(kc p) c -> p kc c", p=128)
        for c0 in (CTX_ORDER if ctx_only else INP_ORDER):
            lst.append(("in", l, c0, wv[:, :, c0 * 128:c0 * 128 + SW]))
        if ctx_only:
            return lst
        wv = W[l]["w_out"].rearrange("(kc p) c -> p kc c", p=128)
        for c0 in range(0, 16, 2):
            lst.append(("out", l, c0, wv[:, :, c0 * 128:c0 * 128 + SW]))
        w1 = W[l]["w_ff1"].rearrange("(kc p) c -> p kc c", p=128)
        w2 = W[l]["w_ff2"].rearrange("(kc p) c -> p kc c", p=128)
        for q in range(4):
            for c0 in range(0, 16, 2):
                f0 = q * 16 + c0
                lst.append(("ff1", l, f0, w1[:, :, f0 * 128:f0 * 128 + SW]))
            for c0 in range(0, 16, 2):
                lst.append(("ff2", l, (q, c0), w2[:, q * 16:(q + 1) * 16, c0 * 128:c0 * 128 + SW]))
        return lst

    slab_list = []
    for u in units:
        if u["kind"] == "halo":
            slab_list += layer_slabs(layers[0], False) + layer_slabs(layers[1], True)
        elif u["kind"] == "ctx":
            slab_list += layer_slabs(layers[0], True)
        else:
            for l in layers:
                slab_list += layer_slabs(l, False)
    slab_state = dict(next=0, issued=0)

    def issue_slabs(upto):
        while slab_state["issued"] <= upto and slab_state["issued"] < len(slab_list):
            j = slab_state["issued"]
            slot = j % NSLOT
            key = ("slab", slot)
            S.deps("pool", [], [key])
            sem = S.dsem[f"slab{slot}"]
            for k2 in range(0, 16, SPLIT):
                ins = nc.gpsimd.dma_start(out=slabs[slot][:, k2:k2 + SPLIT, :], in_=slab_list[j][3][:, k2:k2 + SPLIT, :])
                sem[1] += 16
                ins.then_inc(sem[0], 16)
            S.record((f"slab{slot}", sem[1]), [], [key])
            slab_state["issued"] += 1

    def get_slab(kind, l, tag):
        i = slab_state["next"]
        d = slab_list[i]
        assert d[0] == kind and d[1] == l and d[2] == tag, (d[:3], kind, l, tag)
        slab_state["next"] += 1
        issue_slabs(i + NSLOT - 1)
        return i % NSLOT

    setup_keys = []

    def setup_dma(eng, out, in_, key):
        sn = "setup" if eng == "sp" else "setup_p"
        S.dma(eng, sn, out, in_, [], [key])
        setup_keys.append((key, sn))

    setup_dma("sp", ident[:], ident_d, "ident")
    setup_dma("sp", rc16[:], rc16_d, "rc16")
    setup_dma("sp", cmask[:], cmask_d, "cmask")
    for l in layers:
        setup_dma("sp", pcols[l][:], W[l]["pcols"], ("pcols", l))
    setup_dma("pool", maskT[:], maskT_d, "maskT")
    for k, sn in setup_keys:
        S.lastw[k] = (sn, S.dsem[sn][1])
    S.op("dve", [], ["ones"], lambda e: e.memset(ones[:], 1.0))
    S.op("dve", [], ["epsc"], lambda e: e.memset(epsc[:], EPS))
    S.op("dve", [], ["zc"], lambda e: e.memset(zc[:], 0.0))
    for l in layers:
        for (src, dst) in ((C_L1G, 0), (C_L1B, 16), (C_L2G, 32), (C_L2B, 48)):
            S.op("dve", [("pcols", l)], [("pca", l)],
                 lambda e, l=l, src=src, dst=dst: e.tensor_scalar(
                     out=pca[l][:, dst:dst + 16], in0=pcols[l][:, src:src + 16],
                     scalar1=ALPHA, scalar2=None, op0=ALU.mult))
    issue_slabs(NSLOT - 1)

    mixc_state = dict(layer=None)

    def load_mix_consts(l):
        if mixc_state["layer"] == l:
            return
        mixc_state["layer"] = l
        S.dma("sp", "mixc", rows[:], W[l]["rows"], [], ["rows"])
        S.dma("sp", "mixc", bsb[:], W[l]["bsb"], [], ["bsb"])
        S.dma("pool", "mixc_p", wTraw[:], W[l]["wT"], [], ["wTraw"])
        S.dma("pool", "mixc_p", wpool[:], W[l]["wpool"], [], ["wpool"])
        for k in ("rows", "bsb"):
            S.lastw[k] = ("mixc", S.dsem["mixc"][1])
        for k in ("wTraw", "wpool"):
            S.lastw[k] = ("mixc_p", S.dsem["mixc_p"][1])
        S.op("dve", ["wTraw", "maskT"], ["wTm"],
             lambda e: e.tensor_tensor(out=wTm[:], in0=wTraw[:],
                                       in1=maskT[:].unsqueeze(1).broadcast_to([128, 6, 128]), op=ALU.mult))

    def pc(l, col):
        return pcols[l][:, col:col + 1]

    def ln_fm(srcs, Dn, N, out_fn, ring=False):
        bs_, bq_ = bank(), bank()
        n = len(srcs)
        if ring:
            zbr = [(("xT", j), xT[:, j, 0:N]) for j in range(8)]
            zqr = [(("xT", 8 + j), xT[:, 8 + j, 0:N]) for j in range(8)]
        else:
            zbr = [("zb", zb[:, 0:N])]
            zqr = [("zsq", zsq[:, 0:N])]
        depth = len(zbr)

        def conv_ops(i):
            k, ap = srcs[i]
            zk, zap = zbr[i % depth]
            qk, qap = zqr[i % depth]
            S.op("act", [k], [zk], lambda e: e.activation(out=zap, in_=ap, func=AF.Copy, bias=0.0, scale=1.0))
            S.op("act", [k], [qk], lambda e: e.activation(out=qap, in_=ap, func=AF.Square, bias=0.0, scale=1.0))

        def mm_ops(i):
            zk, zap = zbr[i % depth]
            qk, qap = zqr[i % depth]
            S.mm(ps[bs_][:, 0:N], ones[:], zap, [zk, "ones"], [("ps", bs_)], start=(i == 0), stop=(i == n - 1))
            S.pe_end()
            S.mm(ps[bq_][:, 0:N], ones[:], qap, [qk, "ones"], [("ps", bq_)], start=(i == 0), stop=(i == n - 1))
            S.pe_end()

        for i in range(min(depth, n)):
            conv_ops(i)
        for i in range(n):
            mm_ops(i)
            if i + depth < n:
                conv_ops(i + depth)
        inv = 1.0 / Dn
        S.op("dve", [("ps", bs_)], ["stA"], lambda e: e.tensor_scalar(out=stA[:, 0:N], in0=ps[bs_][:, 0:N], scalar1=inv, scalar2=None, op0=ALU.mult))
        S.op("dve", [("ps", bq_)], ["stB"], lambda e: e.tensor_scalar(out=stB[:, 0:N], in0=ps[bq_][:, 0:N], scalar1=inv, scalar2=None, op0=ALU.mult))
        S.op("dve", ["stA"], ["stC"], lambda e: e.tensor_tensor(out=stC[:, 0:N], in0=stA[:, 0:N], in1=stA[:, 0:N], op=ALU.mult))
        S.op("dve", ["stB", "stC"], ["stB"], lambda e: e.tensor_tensor(out=stB[:, 0:N], in0=stB[:, 0:N], in1=stC[:, 0:N], op=ALU.subtract))
        S.op("act", ["stB", "epsc"], ["stB"], lambda e: e.activation(out=stB[:, 0:N], in_=stB[:, 0:N], func=AF.Sqrt, bias=epsc[:, 0:1], scale=1.0))
        S.op("dve", ["stB"], ["stB"], lambda e: e.reciprocal(out=stB[:, 0:N], in_=stB[:, 0:N]))
        S.op("dve", ["stA", "stB"], ["stA"], lambda e: e.scalar_tensor_tensor(out=stA[:, 0:N], in0=stA[:, 0:N], scalar=-1.0, in1=stB[:, 0:N], op0=ALU.mult, op1=ALU.mult))
        for i, (k, ap) in enumerate(srcs):
            S.op("dve", [k, "stB"], [k], lambda e, ap=ap: e.tensor_tensor(out=ap, in0=ap, in1=stB[:, 0:N], op=ALU.mult))
        for i, (k, ap) in enumerate(srcs):
            S.op("dve", [k, "stA"], [k], lambda e, ap=ap: e.tensor_tensor(out=ap, in0=ap, in1=stA[:, 0:N], op=ALU.add))
        for i in range(n):
            out_fn(i)

    def load_x(u):
        ntok = u["cx"] + u["N"]
        t0 = 0
        while t0 < ntok:
            tn = min(128, ntok - t0)
            for h in range(2):
                S.dma("sp", f"st{h}", stage[h][0:tn, :], x_in[u["row0"] + t0:u["row0"] + t0 + tn, h * 1024:(h + 1) * 1024],
                      [], [("stage", h)])
                for b4 in range(2):
                    b = bank()
                    c0 = h * 8 + b4 * 4
                    for j in range(4):
                        S.mm(ps[b][:, j * 128:j * 128 + tn], stage[h][0:tn, (b4 * 4 + j) * 128:(b4 * 4 + j + 1) * 128],
                             ident[0:tn, 0:tn], [("stage", h), "ident"], [("ps", b)])
                    S.pe_end()
                    pv = ps[b][:].rearrange("p (j t) -> p j t", j=4)[:, :, 0:tn]
                    ck = [("xres", c0 + j) for j in range(4)]
                    tk = [("xT", c0 + j) for j in range(4)]
                    S.op("act", [("ps", b)], tk, lambda e, pv=pv, c0=c0, t0=t0, tn=tn: e.activation(
                        out=xT[:, c0:c0 + 4, t0:t0 + tn], in_=pv, func=AF.Copy, bias=0.0, scale=1.0))
                    lo = max(t0, u["cx"])
                    if lo < t0 + tn:
                        pvr = ps[b][:].rearrange("p (j t) -> p j t", j=4)[:, :, lo - t0:tn]
                        S.op("dve", [("ps", b)] + tk, ck, lambda e, pvr=pvr, c0=c0, lo=lo, t0=t0, tn=tn: e.tensor_scalar(
                            out=xres[:, c0:c0 + 4, lo - u["cx"]:t0 + tn - u["cx"]], in0=pvr, scalar1=ALPHA, scalar2=None, op0=ALU.mult))
            t0 += tn

    def store_out(u):
        N = u["N"]
        orow = u["row0"] - cxh
        for tt in range(N // 128):
            for h in range(2):
                for b4 in range(2):
                    b = bank()
                    c0 = h * 8 + b4 * 4
                    for j in range(4):
                        S.mm(ps[b][:, j * 128:(j + 1) * 128], xres[:, c0 + j, tt * 128:(tt + 1) * 128], ident[:],
                             [("xres", c0 + j), "ident"], [("ps", b)])
                    S.pe_end()
                    S.op("act", [("ps", b)], [("stage", h)], lambda e, b=b, h=h, b4=b4: e.activation(
                        out=stage[h][:, b4 * 512:(b4 + 1) * 512], in_=ps[b][:], func=AF.Copy, bias=0.0, scale=1.0))
                S.dma("sp", f"out{h}", out_d[orow + tt * 128:orow + (tt + 1) * 128, h * 1024:(h + 1) * 1024], stage[h][:],
                      [("stage", h)], [("outd", tt, h, u["g"])])

    def proj_batch(slot, j, src, N, col0, nk=16):
        b = bank()
        for kc in range(nk):
            S.mm(ps[b][:, 0:N], slabs[slot][:, kc, j * 128:(j + 1) * 128], src[:, kc, col0:col0 + N],
                 [("slab", slot), (src_key[id(src)], kc)], [("ps", b)], start=(kc == 0), stop=(kc == nk - 1))
        S.pe_end()
        return b

    src_key = {id(xT): "xT", id(mT): "mT"}

    def mix_inproj(l, u, ctx_only):
        N, cx = u["N"], u["cx"]
        if ctx_only:
            ncol, xcol0, dcol0 = CX, u["xcol0"], CX
        else:
            ncol, xcol0, dcol0 = cx + N, 0, CX - cx
        if not ctx_only:
            load_mix_consts(l)
            if cx == 0:
                S.op("dve", [("pactx", l)], [("pa", i) for i in range(4)],
                     lambda e: e.tensor_copy(out=pa[:, :, 0:CX], in_=pactx[l][:]))
                S.op("dve", [("hctx", l)], [("hglu", i) for i in range(6)],
                     lambda e: e.tensor_copy(out=hglu[:, :, 0:CX], in_=hctx[l][:]))
        order = CTX_ORDER if ctx_only else INP_ORDER
        for c0 in order:
            slot = get_slab("in", l, c0)
            if 10 <= c0 < 16:
                vc0 = (c0 - 10) * 128
                for tt in range(N // 128):
                    b = bank()
                    for kc in range(16):
                        S.mm(ps[b][:, 0:SW], xT[:, kc, cx + tt * 128:cx + (tt + 1) * 128], slabs[slot][:, kc, :],
                             [("slab", slot), ("xT", kc)], [("ps", b)], start=(kc == 0), stop=(kc == 15))
                    S.pe_end()
                    S.op("act", [("ps", b)], [("vraw", tt)], lambda e, b=b, tt=tt, vc0=vc0: e.activation(
                        out=vraw[:, tt, vc0:vc0 + SW], in_=ps[b][:, 0:SW], func=AF.Copy, bias=0.0, scale=1.0))
                if c0 == 14:
                    sgu_path(l, u)
                    conv_ln(l, u)
                continue
            for j in range(2):
                c = c0 + j
                if c < 4:
                    b = proj_batch(slot, j, xT, ncol, xcol0)
                    S.op("act", [("ps", b), ("pcols", l)], [("pa", c)], lambda e, b=b, c=c: e.activation(
                        out=pa[:, c, dcol0:dcol0 + ncol], in_=ps[b][:, 0:ncol], func=AF.Identity, bias=pc(l, C_BIN + c), scale=1.0))
                elif c < 10:
                    i = c - 4
                    b = proj_batch(slot, j, xT, N, cx)
                    S.op("act", [("ps", b), ("pcols", l)], [("ub", i)], lambda e, b=b, c=c, i=i: e.activation(
                        out=ub[:, i, 0:N], in_=ps[b][:, 0:N], func=AF.Gelu_apprx_tanh, bias=pc(l, C_BIN + c), scale=1.0))
                elif c < 22:
                    i = c - 16
                    b = proj_batch(slot, j, xT, ncol, xcol0)
                    S.op("act", [("ps", b), ("pcols", l)], [("hglu", i)], lambda e, b=b, c=c, i=i: e.activation(
                        out=hglu[:, i, dcol0:dcol0 + ncol], in_=ps[b][:, 0:ncol], func=AF.Identity, bias=pc(l, C_BIN + c), scale=1.0))
                else:
                    i = c - 22
                    b = proj_batch(slot, j, xT, ncol, xcol0)
                    sg = sig[i % 2]
                    sk = ("sig", i % 2)
                    S.op("act", [("ps", b), ("pcols", l)], [sk], lambda e, b=b, c=c, sg=sg: e.activation(
                        out=sg[:, 0:ncol], in_=ps[b][:, 0:ncol], func=AF.Sigmoid, bias=pc(l, C_BIN + c), scale=1.0))
                    S.op("dve", [sk, ("hglu", i)], [("hglu", i)], lambda e, i=i, sg=sg: e.tensor_tensor(
                        out=hglu[:, i, dcol0:dcol0 + ncol], in0=hglu[:, i, dcol0:dcol0 + ncol], in1=sg[:, 0:ncol], op=ALU.mult))
            if c0 == 2:
                if ctx_only:
                    save_ctx_pa(l, CX, True)
                else:
                    pool_path(l, u)
            if c0 == 20 and not ctx_only:
                pool_pe(l, u)
            if c0 == 26:
                if ctx_only:
                    save_ctx_h(l, CX, True)
                else:
                    conv_path(l, u)

    def save_ctx_pa(l, N, masked):
        rk = [("pa", i) for i in range(4)]
        if masked:
            S.op("dve", rk + ["cmask"], [("pactx", l)], lambda e: e.tensor_scalar(
                out=pactx[l][:], in0=pa[:, :, N:N + CX], scalar1=cmask[:, 0:1], scalar2=None, op0=ALU.mult))
        else:
            S.op("dve", rk, [("pactx", l)], lambda e: e.tensor_copy(out=pactx[l][:], in_=pa[:, :, N:N + CX]))

    def save_ctx_h(l, N, masked):
        rk = [("hglu", i) for i in range(6)]
        if masked:
            S.op("dve", rk + ["cmask"], [("hctx", l)], lambda e: e.tensor_scalar(
                out=hctx[l][:], in0=hglu[:, :, N:N + CX], scalar1=cmask[:, 0:1], scalar2=None, op0=ALU.mult))
        else:
            S.op("dve", rk, [("hctx", l)], lambda e: e.tensor_copy(out=hctx[l][:], in_=hglu[:, :, N:N + CX]))

    pbufs = [(pooled[0], ("pooled", 0)), (pooled[1], ("pooled", 1)), (zb, "zb"), (zsq, "zsq")]

    def pool_path(l, u):
        N = u["N"]
        first_main = (u["kind"] == "main" and u["g"] == 0)
        E = CX + N
        for gi, win in enumerate(WINS):
            a = pa[:, gi, :]
            ak = ("pa", gi)
            bufs = [(sA, "sA"), (sB, "sB")]
            src, srck = a, ak
            lo = CX - (win - 1)
            step = 1
            bi = 0
            while step < win:
                dst, dstk = bufs[bi]
                lo2 = lo + step
                S.op("dve", [srck], [dstk], lambda e, dst=dst, src=src, lo2=lo2, step=step: e.tensor_tensor(
                    out=dst[:, lo2:E], in0=src[:, lo2:E], in1=src[:, lo2 - step:E - step], op=ALU.add))
                src, srck = dst, dstk
                lo = lo2
                step *= 2
                bi ^= 1
            pb, pk = pbufs[gi]
            S.op("dve", [srck, ak], [pk], lambda e, pb=pb, src=src, a=a, win=win: e.scalar_tensor_tensor(
                out=pb[:, 0:N], in0=src[:, CX:E], scalar=1.0 / win, in1=a[:, CX:E], op0=ALU.mult, op1=ALU.subtract))
            if first_main:
                S.op("dve", [srck, "rc16"], ["stC"], lambda e, src=src, gi=gi: e.tensor_tensor(
                    out=stC[:, 0:16], in0=src[:, CX:CX + 16], in1=rc16[:, gi, :], op=ALU.mult))
                S.op("dve", ["stC", ak], [pk], lambda e, pb=pb, a=a: e.tensor_tensor(
                    out=pb[:, 0:16], in0=stC[:, 0:16], in1=a[:, CX:CX + 16], op=ALU.subtract))
        save_ctx_pa(l, N, u["kind"] == "halo")

    def pool_pe(l, u):
        N = u["N"]
        for gi in range(4):
            pb, pk = pbufs[gi]
            b = bank()
            S.mm(ps[b][:, 0:N], wpool[:, gi, :], pb[:, 0:N], ["wpool", pk], [("ps", b)])
            S.pe_end()
            S.op("act", [("ps", b), ("pcols", l), "zc"], [("mT", gi)], lambda e, b=b, gi=gi: e.activation(
                out=mT[:, gi, 0:N], in_=ps[b][:, 0:N], func=AF.Identity, bias=zc[:, 0:1], scale=pc(l, C_PSC + gi)))

    def conv_path(l, u):
        N = u["N"]
        for k in range(31):
            for i in range(6):
                wcol = pcols[l][:, C_CW + i * 31 + k:C_CW + i * 31 + k + 1]
                if k == 0:
                    S.op("dve", [("hglu", i), ("pcols", l)], [("acc", i)], lambda e, i=i, wcol=wcol: e.tensor_scalar(
                        out=acc[:, i, 0:N], in0=hglu[:, i, 2:2 + N], scalar1=wcol, scalar2=pc(l, C_CB + i),
                        op0=ALU.mult, op1=ALU.add))
                else:
                    S.op("dve", [("hglu", i), ("pcols", l), ("acc", i)], [("acc", i)], lambda e, i=i, wcol=wcol, k=k: e.scalar_tensor_tensor(
                        out=acc[:, i, 0:N], in0=hglu[:, i, 2 + k:2 + k + N], scalar=wcol, in1=acc[:, i, 0:N],
                        op0=ALU.mult, op1=ALU.add))
        save_ctx_h(l, N, u["kind"] == "halo")

    def conv_ln(l, u):
        N = u["N"]

        def outf(i):
            S.op("act", [("acc", i), ("pcols", l)], [("mT", 10 + i)], lambda e, i=i: e.activation(
                out=mT[:, 10 + i, 0:N], in_=acc[:, i, 0:N], func=AF.Silu, bias=pc(l, C_CLB + i), scale=pc(l, C_CLG + i)))
        ln_fm([(("acc", i), acc[:, i, 0:N]) for i in range(6)], 768.0, N, outf)

    def sgu_path(l, u):
        N = u["N"]
        ntt = N // 128
        tts = list(range(ntt))
        V = {tt: vraw[:, tt, :] for tt in tts}
        VK = {tt: ("vraw", tt) for tt in tts}
        for tt in tts:
            S.op("dve", [VK[tt], "rows"], [VK[tt]], lambda e, v=V[tt]: e.tensor_tensor(out=v, in0=v, in1=rows[:, 0, :], op=ALU.add))
        for tt in tts:
            S.op("act", [VK[tt]], [VK[tt]], lambda e, v=V[tt]: e.activation(out=v, in_=v, func=AF.Gelu_apprx_tanh, bias=0.0, scale=1.0))
        for tt in tts:
            S.op("dve", [VK[tt]], [("bnst", tt)], lambda e, v=V[tt], tt=tt: e.bn_stats(out=bnst4[:, tt, 0, :], in_=v[:, 0:384]))
            S.op("dve", [VK[tt]], [("bnst", tt)], lambda e, v=V[tt], tt=tt: e.bn_stats(out=bnst4[:, tt, 1, :], in_=v[:, 384:768]))
        for tt in tts:
            S.op("dve", [("bnst", tt)], [("bnag", tt)], lambda e, tt=tt: e.bn_aggr(out=bnag4[:, tt, 0:2], in_=bnst4[:, tt, :, :]))
        for tt in tts:
            S.op("act", [("bnag", tt), "epsc"], [("bnag2", tt)], lambda e, tt=tt: e.activation(
                out=bnag4[:, tt, 2:3], in_=bnag4[:, tt, 1:2], func=AF.Sqrt, bias=epsc[:, 0:1], scale=1.0))
        for tt in tts:
            S.op("dve", [("bnag2", tt)], [("bnag3", tt)], lambda e, tt=tt: e.reciprocal(out=bnag4[:, tt, 3:4], in_=bnag4[:, tt, 2:3]))
        for tt in tts:
            S.op("dve", [VK[tt], ("bnag", tt), ("bnag3", tt)], [VK[tt]], lambda e, v=V[tt], tt=tt: e.tensor_scalar(
                out=v, in0=v, scalar1=bnag4[:, tt, 0:1], scalar2=bnag4[:, tt, 3:4], op0=ALU.subtract, op1=ALU.mult))
        for tt in tts:
            S.op("dve", [VK[tt], "rows"], [VK[tt]], lambda e, v=V[tt]: e.tensor_tensor(out=v, in0=v, in1=rows[:, 1, :], op=ALU.mult))
        for tt in tts:
            v = V[tt]
            vb = vnb[tt % 2]
            vbk = ("vnb", tt % 2)
            S.op("dve", [VK[tt], "rows"], [vbk], lambda e, v=v, vb=vb: e.tensor_tensor(out=vb[:], in0=v, in1=rows[:, 2, :], op=ALU.add))
            for h in range(6):
                b = bank()
                S.mm(ps[b][:, 0:128], vb[:, h * 128:(h + 1) * 128], wTm[:, h, :], [vbk, "wTm"], [("ps", b)])
                S.pe_end()
                S.op("dve", [("ps", b), "bsb"], ["stC"], lambda e, b=b, h=h: e.tensor_tensor(
                    out=stC[:, 0:128], in0=ps[b][:, 0:128], in1=bsb[:, h, :], op=ALU.add))
                S.op("dve", ["stC", ("ub", h)], [("mT", 4 + h)], lambda e, h=h, tt=tt: e.tensor_tensor(
                    out=mT[:, 4 + h, tt * 128:(tt + 1) * 128], in0=stC[:, 0:128], in1=ub[:, h, tt * 128:(tt + 1) * 128], op=ALU.mult))

    def out_proj_ln1(l, u):
        N = u["N"]
        for c0 in range(0, 16, 2):
            slot = get_slab("out", l, c0)
            for j in range(2):
                c = c0 + j
                b = proj_batch(slot, j, mT, N, 0)
                S.op("dve", [("ps", b), ("pcols", l), ("xres", c)], [("xres", c)], lambda e, b=b, c=c: e.scalar_tensor_tensor(
                    out=xres[:, c, 0:N], in0=ps[b][:, 0:N], scalar=pc(l, C_BOUT + c), in1=xres[:, c, 0:N], op0=ALU.add, op1=ALU.add))

        def outf(i):
            S.op("act", [("xres", i), ("pcols", l)], [("xT", i)], lambda e, i=i: e.activation(
                out=xT[:, i, 0:N], in_=xres[:, i, 0:N], func=AF.Identity, bias=pc(l, C_L1B + i), scale=pc(l, C_L1G + i)))
            S.op("dve", [("xres", i), ("pca", l)], [("xres", i)], lambda e, i=i: e.tensor_scalar(
                out=xres[:, i, 0:N], in0=xres[:, i, 0:N], scalar1=pca[l][:, i:i + 1], scalar2=pca[l][:, 16 + i:17 + i],
                op0=ALU.mult, op1=ALU.add))
        ln_fm([(("xres", i), xres[:, i, 0:N]) for i in range(16)], float(D), N, outf, ring=True)

    def ffn_ln2(l, u, final, need_res):
        N = u["N"]
        for q in range(4):
            for c0 in range(0, 16, 2):
                f0 = q * 16 + c0
                slot = get_slab("ff1", l, f0)
                for j in range(2):
                    f = f0 + j
                    b = proj_batch(slot, j, xT, N, 0)
                    sg = sig[f % 2]
                    sk = ("sig", f % 2)
                    S.op("act", [("ps", b), ("pcols", l)], [sk], lambda e, b=b, f=f, sg=sg: e.activation(
                        out=sg[:, 0:N], in_=ps[b][:, 0:N], func=AF.Relu, bias=pc(l, C_BF1 + f), scale=1.0))
                    S.op("dve", [sk], [("mT", c0 + j)], lambda e, sg=sg, c0=c0, j=j: e.tensor_tensor(
                        out=mT[:, c0 + j, 0:N], in0=sg[:, 0:N], in1=sg[:, 0:N], op=ALU.mult))
            for c0 in range(0, 16, 2):
                slot = get_slab("ff2", l, (q, c0))
                for j in range(2):
                    c = c0 + j
                    b = proj_batch(slot, j, mT, N, 0)
                    if q == 0:
                        S.op("dve", [("ps", b), ("pcols", l), ("xres", c)], [("xres", c)], lambda e, b=b, c=c: e.scalar_tensor_tensor(
                            out=xres[:, c, 0:N], in0=ps[b][:, 0:N], scalar=pc(l, C_BF2 + c), in1=xres[:, c, 0:N], op0=ALU.add, op1=ALU.add))
                    else:
                        S.op("dve", [("ps", b), ("xres", c)], [("xres", c)], lambda e, b=b, c=c: e.tensor_tensor(
                            out=xres[:, c, 0:N], in0=ps[b][:, 0:N], in1=xres[:, c, 0:N], op=ALU.add))

        def outf(i):
            if final:
                S.op("dve", [("xres", i), ("pcols", l)], [("xres", i)], lambda e, i=i: e.tensor_scalar(
                    out=xres[:, i, 0:N], in0=xres[:, i, 0:N], scalar1=pc(l, C_L2G + i), scalar2=pc(l, C_L2B + i),
                    op0=ALU.mult, op1=ALU.add))
            else:
                S.op("act", [("xres", i), ("pcols", l)], [("xT", i)], lambda e, i=i: e.activation(
                    out=xT[:, i, 0:N], in_=xres[:, i, 0:N], func=AF.Identity, bias=pc(l, C_L2B + i), scale=pc(l, C_L2G + i)))
                if need_res:
                    S.op("dve", [("xres", i), ("pca", l)], [("xres", i)], lambda e, i=i: e.tensor_scalar(
                        out=xres[:, i, 0:N], in0=xres[:, i, 0:N], scalar1=pca[l][:, 32 + i:33 + i], scalar2=pca[l][:, 48 + i:49 + i],
                        op0=ALU.mult, op1=ALU.add))
        ln_fm([(("xres", i), xres[:, i, 0:N]) for i in range(16)], float(D), N, outf, ring=True)

    import os
    KSTOP = int(os.environ.get("KSTOP", "0"))
    class StopEmit(Exception):
        pass
    stage_ctr = [0]
    def ckpt():
        stage_ctr[0] += 1
        if KSTOP and stage_ctr[0] >= KSTOP:
            raise StopEmit()
    try:
      for u in units:
        load_x(u)
        ckpt()
        if u["kind"] == "ctx":
            u2 = dict(u)
            u2["xcol0"] = 0
            mix_inproj(layers[0], u2, True)
            ckpt()
            continue
        if u["kind"] == "halo":
            l = layers[0]
            mix_inproj(l, u, False)
            out_proj_ln1(l, u)
            ffn_ln2(l, u, False, False)
            u2 = dict(u)
            u2["xcol0"] = u["N"] - CX
            mix_inproj(layers[1], u2, True)
            continue
        for li, l in enumerate(layers):
            last = (li == len(layers) - 1)
            mix_inproj(l, u, False)
            ckpt()
            out_proj_ln1(l, u)
            ckpt()
            ffn_ln2(l, u, last, True)
            ckpt()
        store_out(u)
        ckpt()
      assert slab_state["next"] == len(slab_list), (slab_state, len(slab_list))
    except StopEmit:
        if S.pe_open:
            S.pe_end()
        for k in ("pe", "act", "dve", "pool"):
            if S.cnt[k]:
                nc.sync.wait_ge(S.sem[k], S.cnt[k])
        for k, v in S.dsem.items():
            if v[1]:
                nc.sync.wait_ge(v[0], v[1])
    for h in range(2):
        nc.sync.wait_ge(S.dsem[f"out{h}"][0], S.dsem[f"out{h}"][1])
    es.close()
    return nc


def _host_layer_inputs(inp, l):
    f = np.float32
    cols = []
    cols.append(inp["b_in"][l].reshape(28, 128).T)
    cols.append(inp["pool_scale"][l].reshape(4, 128).T)
    cols.append(inp["conv_w"][l].T.reshape(6, 128, 31).transpose(1, 0, 2).reshape(128, 186))
    for n, k in (("conv_b", 6), ("conv_ln_g", 6), ("conv_ln_b", 6), ("b_out", 16), ("ln1_g", 16), ("ln1_b", 16),
                 ("b_ff1", 64), ("b_ff2", 16), ("ln2_g", 16), ("ln2_b", 16)):
        cols.append(inp[n][l].reshape(k, 128).T)
    pcols = np.ascontiguousarray(np.concatenate(cols, axis=1), dtype=f)
    assert pcols.shape == (128, NPC)
    rows = np.stack([inp["b_in"][l][1280:2048], inp["sgu_ln_g"][l], inp["sgu_ln_b"][l]], axis=0)
    rows = np.ascontiguousarray(np.broadcast_to(rows[None], (128, 3, 768)), dtype=f)
    bsb = np.ascontiguousarray(np.broadcast_to(inp["sgu_b"][l][None], (128, 6, 128)), dtype=f)
    wT = np.ascontiguousarray(inp["sgu_w"][l].transpose(2, 0, 1), dtype=f)
    wpool = np.ascontiguousarray(inp["w_pool"][l].transpose(1, 0, 2), dtype=f)
    return {
        f"w_in{l}": np.ascontiguousarray(inp["w_in"][l]), f"w_out{l}": np.ascontiguousarray(inp["w_out"][l]),
        f"w_ff1{l}": np.ascontiguousarray(inp["w_ff1"][l]), f"w_ff2{l}": np.ascontiguousarray(inp["w_ff2"][l]),
        f"pcols{l}": pcols, f"rows{l}": rows, f"bsb{l}": bsb, f"wT{l}": wT, f"wpool{l}": wpool,
    }


def _core_consts(seg):
    ident = np.eye(128, dtype=np.float32)
    maskT = np.triu(np.ones((128, 128), dtype=np.float32))
    rc = np.zeros((128, 4, 16), dtype=np.float32)
    for gi, w in enumerate(WINS):
        for j in range(16):
            rc[:, gi, j] = 1.0 / (min(j + 1, w) if seg == 0 else w)
    cm = np.full((128, 1), 0.0 if seg == 0 else 1.0, dtype=np.float32)
    return {"ident": ident, "maskT": maskT, "rc16": rc, "cmask": cm}


def _run(layers, cxh, xfull, inp):
    nc = build_program(layers, cxh)
    shared = {}
    for l in layers:
        shared.update(_host_layer_inputs(inp, l))
    in_maps = []
    for c in range(NCORES):
        b, seg = divmod(c, 4)
        s0 = seg * TOK
        xc = np.zeros((cxh + TOK, D), dtype=np.float32)
        xc[cxh:] = xfull[b, s0:s0 + TOK]
        if seg > 0:
            xc[:cxh] = xfull[b, s0 - cxh:s0]
        m = dict(shared)
        m.update(_core_consts(seg))
        m["x_core"] = xc
        in_maps.append(m)
    res = run_bass_kernel_spmd(nc, in_maps, core_ids=list(range(NCORES)))
    out = np.zeros_like(xfull)
    for c in range(NCORES):
        b, seg = divmod(c, 4)
        out[b, seg * TOK:(seg + 1) * TOK] = res.results[c]["out_core"]
    return out


def kernel(**inputs):
    inp = {k: np.asarray(v) for k, v in inputs.items()}
    x = np.ascontiguousarray(inp["x"], dtype=np.float32)
    if FUSED:
        return _run([0, 1], HALO, x, inp)
    x1 = _run([0], CX, x, inp)
    return _run([1], CX, x1, inp)
```
